# Optimizing a Trainium2 kernel written in Bass

```python
import math
import jax, jax.numpy as jnp
from jax import lax
import numpy as np

D_MODEL = 2048
BATCH = 2
SEQ = 4096
DEPTH = 1

DA_HEADS = 8
DA_HEAD_DIM = 64
DA_V_DIM = 2 * DA_HEAD_DIM
DA_QK = 2 * DA_HEADS * DA_HEAD_DIM
DA_V = DA_HEADS * DA_V_DIM
GDN_HEADS = 8
GDN_K_DIM = 128
GDN_V_DIM = 128
GDN_QK = GDN_HEADS * GDN_K_DIM
GDN_V = GDN_HEADS * GDN_V_DIM
GDN_CONV_CH = 2 * GDN_QK + GDN_V
CONV_WIDTH = 4
CHUNK = 64
MIX_WIDTH = DA_V + GDN_V
IN_COLS = 2 * DA_QK + DA_V + GDN_CONV_CH + GDN_V + 2 * GDN_HEADS
D_FF = 4 * D_MODEL
Q_BLOCK = 128
ROPE_THETA = 10000.0
EPS = 1e-6
NEG_INF = -1e30

kernel_name = "hymba_diffattn_gdn_sqrelu_sandwich"


def rmsnorm(x, w):
    xf = x.astype(jnp.float32)
    y = xf * lax.rsqrt(jnp.mean(xf * xf, axis=-1, keepdims=True) + EPS)
    return (y * w.astype(jnp.float32)).astype(x.dtype)


def rotary(x):
    s, d = x.shape[1], x.shape[-1]
    inv_freq = ROPE_THETA ** (-jnp.arange(0, d, 2, dtype=jnp.float32) / d)
    ang = jnp.arange(s, dtype=jnp.float32)[:, None] * inv_freq[None, :]
    cos = jnp.cos(ang)[None, :, None, :]
    sin = jnp.sin(ang)[None, :, None, :]
    xf = x.astype(jnp.float32)
    x1, x2 = xf[..., : d // 2], xf[..., d // 2:]
    out = jnp.concatenate([x1 * cos - x2 * sin, x2 * cos + x1 * sin], axis=-1)
    return out.astype(x.dtype)


def diff_attention(q, k, v, lam, subln_w, lambda_init):
    b, s, h2, d = q.shape
    n_blocks = s // Q_BLOCK
    scale = d ** -0.5
    kf = k.astype(jnp.float32)
    vf = v.astype(jnp.float32)
    qb = jnp.moveaxis(q.astype(jnp.float32).reshape(b, n_blocks, Q_BLOCK, h2, d), 1, 0)
    key_pos = jnp.arange(s)

    def block(args):
        q_blk, blk = args
        scores = jnp.einsum('bqhd,bkhd->bhqk', q_blk, kf) * scale
        q_pos = blk * Q_BLOCK + jnp.arange(Q_BLOCK)
        causal = key_pos[None, :] <= q_pos[:, None]
        scores = jnp.where(causal, scores, NEG_INF)
        p = jax.nn.softmax(scores, axis=-1).reshape(b, h2 // 2, 2, Q_BLOCK, s)
        diff = p[:, :, 0] - lam * p[:, :, 1]
        return jnp.einsum('bhqk,bkhe->bqhe', diff, vf)

    out = lax.map(block, (qb, jnp.arange(n_blocks)))
    out = jnp.moveaxis(out, 0, 1).reshape(b, s, h2 // 2, 2 * d)
    out = rmsnorm(out, subln_w) * (1.0 - lambda_init)
    return out.astype(q.dtype).reshape(b, s, (h2 // 2) * 2 * d)


def causal_depthwise_conv(x, w):
    c = x.shape[-1]
    return lax.conv_general_dilated(
        x, w[:, None, :].astype(x.dtype), window_strides=(1,),
        padding=[(CONV_WIDTH - 1, 0)], dimension_numbers=('NWC', 'WIO', 'NWC'),
        feature_group_count=c)


def l2norm(x):
    return x * lax.rsqrt(jnp.sum(x * x, axis=-1, keepdims=True) + EPS)


def chunked_gated_delta_rule(q, k, v, beta, g):
    b, s, h, dk = q.shape
    dv = v.shape[-1]
    n = s // CHUNK

    def to_chunks(t):
        t = jnp.moveaxis(t, 2, 1)
        t = t.reshape((b, h, n, CHUNK) + t.shape[3:])
        return jnp.moveaxis(t, 2, 0)

    q, k, v, beta, g = (to_chunks(t) for t in (q, k, v, beta, g))
    g = jnp.cumsum(g, axis=-1)
    idx = jnp.arange(CHUNK)
    incl = idx[:, None] >= idx[None, :]
    strict = idx[:, None] > idx[None, :]
    gdiff = g[..., :, None] - g[..., None, :]
    decay = jnp.where(incl, jnp.exp(jnp.where(incl, gdiff, 0.0)), 0.0)
    k_beta = k * beta[..., None]
    v_beta = v * beta[..., None]
    lower = jnp.where(strict, jnp.einsum('nbhid,nbhjd->nbhij', k_beta, k) * decay, 0.0)
    a = lower + jnp.eye(CHUNK, dtype=jnp.float32)
    u = lax.linalg.triangular_solve(a, v_beta, left_side=True, lower=True, unit_diagonal=True)
    w = lax.linalg.triangular_solve(a, k_beta * jnp.exp(g)[..., None], left_side=True,
                                    lower=True, unit_diagonal=True)
    qk = jnp.einsum('nbhid,nbhjd->nbhij', q, k) * decay
    q_decayed = q * jnp.exp(g)[..., None]
    k_to_end = k * jnp.exp(g[..., -1:] - g)[..., None]
    g_last = jnp.exp(g[..., -1])

    def step(state, xs):
        q_d, qk_c, u_c, w_c, k_e, gl = xs
        v_new = u_c - jnp.einsum('bhcd,bhde->bhce', w_c, state)
        o = (jnp.einsum('bhcd,bhde->bhce', q_d, state)
             + jnp.einsum('bhij,bhje->bhie', qk_c, v_new))
        state = state * gl[..., None, None] + jnp.einsum('bhcd,bhce->bhde', k_e, v_new)
        return state, o

    state0 = jnp.zeros((b, h, dk, dv), jnp.float32)
    _, o = lax.scan(step, state0, (q_decayed, qk, u, w, k_to_end, g_last))
    o = jnp.moveaxis(o, 0, 2).reshape(b, h, s, dv)
    return jnp.moveaxis(o, 1, 2)


def setup_inputs(seed: int = 0) -> dict:
    key = jax.random.key(seed)
    ks = jax.random.split(key, 20)
    f32 = jnp.float32

    def gain(k, n):
        return 1.0 + 0.05 * jax.random.normal(k, (DEPTH, n), f32)

    x = jax.random.normal(ks[0], (BATCH, SEQ, D_MODEL), f32)
    w_in = jax.random.normal(ks[1], (DEPTH, D_MODEL, IN_COLS), f32) * D_MODEL ** -0.5
    conv_w = jax.random.normal(ks[2], (DEPTH, CONV_WIDTH, GDN_CONV_CH), f32) * CONV_WIDTH ** -0.5
    a_log = jnp.log(jax.random.uniform(ks[3], (DEPTH, GDN_HEADS), f32, 1.0, 16.0))
    dt = jnp.exp(jax.random.uniform(ks[4], (DEPTH, GDN_HEADS), f32, math.log(1e-3), math.log(1e-1)))
    dt_bias = dt + jnp.log(-jnp.expm1(-dt))
    gdn_norm_w = gain(ks[5], GDN_V_DIM)
    lambda_q1 = 0.1 * jax.random.normal(ks[6], (DEPTH, DA_HEAD_DIM), f32)
    lambda_k1 = 0.1 * jax.random.normal(ks[7], (DEPTH, DA_HEAD_DIM), f32)
    lambda_q2 = 0.1 * jax.random.normal(ks[8], (DEPTH, DA_HEAD_DIM), f32)
    lambda_k2 = 0.1 * jax.random.normal(ks[9], (DEPTH, DA_HEAD_DIM), f32)
    da_subln_w = gain(ks[10], DA_V_DIM)
    w_out = jax.random.normal(ks[11], (DEPTH, MIX_WIDTH, D_MODEL), f32) * MIX_WIDTH ** -0.5
    w_up = jax.random.normal(ks[12], (DEPTH, D_MODEL, D_FF), f32) * D_MODEL ** -0.5
    w_down = jax.random.normal(ks[13], (DEPTH, D_FF, D_MODEL), f32) * D_FF ** -0.5
    norm_pre_mix = gain(ks[14], D_MODEL)
    norm_post_mix = gain(ks[15], D_MODEL)
    norm_pre_mlp = gain(ks[16], D_MODEL)
    norm_post_mlp = gain(ks[17], D_MODEL)
    return {"x": x, "w_in": w_in, "conv_w": conv_w, "a_log": a_log, "dt_bias": dt_bias,
            "gdn_norm_w": gdn_norm_w, "lambda_q1": lambda_q1, "lambda_k1": lambda_k1,
            "lambda_q2": lambda_q2, "lambda_k2": lambda_k2, "da_subln_w": da_subln_w,
            "w_out": w_out, "w_up": w_up, "w_down": w_down,
            "norm_pre_mix": norm_pre_mix, "norm_post_mix": norm_post_mix,
            "norm_pre_mlp": norm_pre_mlp, "norm_post_mlp": norm_post_mlp}


def reference(x, w_in, conv_w, a_log, dt_bias, gdn_norm_w, lambda_q1, lambda_k1,
              lambda_q2, lambda_k2, da_subln_w, w_out, w_up, w_down,
              norm_pre_mix, norm_post_mix, norm_pre_mlp, norm_post_mlp):
    b, s, _ = x.shape
    split_at = [DA_QK, 2 * DA_QK, 2 * DA_QK + DA_V, 2 * DA_QK + DA_V + GDN_CONV_CH,
                2 * DA_QK + DA_V + GDN_CONV_CH + GDN_V,
                2 * DA_QK + DA_V + GDN_CONV_CH + GDN_V + GDN_HEADS]
    for l in range(DEPTH):
        h = rmsnorm(x, norm_pre_mix[l])
        proj = h @ w_in[l]
        da_q, da_k, da_v, gdn_qkv, gdn_z, gdn_b, gdn_a = jnp.split(proj, split_at, axis=-1)

        lambda_init = 0.8 - 0.6 * math.exp(-0.3 * l)
        lam = (jnp.exp(jnp.sum(lambda_q1[l].astype(jnp.float32) * lambda_k1[l].astype(jnp.float32)))
               - jnp.exp(jnp.sum(lambda_q2[l].astype(jnp.float32) * lambda_k2[l].astype(jnp.float32)))
               + lambda_init)
        dq = rotary(da_q.reshape(b, s, 2 * DA_HEADS, DA_HEAD_DIM))
        dk = rotary(da_k.reshape(b, s, 2 * DA_HEADS, DA_HEAD_DIM))
        dv = da_v.reshape(b, s, DA_HEADS, DA_V_DIM)
        da_out = diff_attention(dq, dk, dv, lam, da_subln_w[l], lambda_init)

        conv = jax.nn.silu(causal_depthwise_conv(gdn_qkv, conv_w[l]))
        gq, gk, gv = jnp.split(conv, [GDN_QK, 2 * GDN_QK], axis=-1)
        gq = l2norm(gq.reshape(b, s, GDN_HEADS, GDN_K_DIM).astype(jnp.float32)) * GDN_K_DIM ** -0.5
        gk = l2norm(gk.reshape(b, s, GDN_HEADS, GDN_K_DIM).astype(jnp.float32))
        gv = gv.reshape(b, s, GDN_HEADS, GDN_V_DIM).astype(jnp.float32)
        beta = jax.nn.sigmoid(gdn_b.astype(jnp.float32))
        g = -jnp.exp(a_log[l].astype(jnp.float32)) * jax.nn.softplus(
            gdn_a.astype(jnp.float32) + dt_bias[l].astype(jnp.float32))
        go = chunked_gated_delta_rule(gq, gk, gv, beta, g)
        z = gdn_z.reshape(b, s, GDN_HEADS, GDN_V_DIM).astype(jnp.float32)
        gdn_out = (rmsnorm(go, gdn_norm_w[l]) * jax.nn.silu(z)).astype(x.dtype).reshape(b, s, GDN_V)

        mix = jnp.concatenate([da_out, gdn_out], axis=-1) @ w_out[l]
        x = x + rmsnorm(mix, norm_post_mix[l])

        h = rmsnorm(x, norm_pre_mlp[l])
        y = jnp.square(jax.nn.relu(h @ w_up[l])) @ w_down[l]
        x = x + rmsnorm(y, norm_post_mlp[l])
    return x
```

```python
import contextlib
import numpy as np
import ml_dtypes
import concourse.bass as bass
import concourse.mybir as mybir
from concourse.bass_utils import run_bass_kernel_spmd

F32 = mybir.dt.float32
BF16 = mybir.dt.bfloat16
F32R = mybir.dt.float32r
ALU = mybir.AluOpType
AF = mybir.ActivationFunctionType
AX = mybir.AxisListType

D = 2048
SEQ = 4096
NB = 8
EPS = 1e-6
LAMBDA_INIT = 0.2
WC = 1796


class Buf:
    __slots__ = ("name", "w", "r")

    def __init__(self, name):
        self.name = name
        self.w = {}
        self.r = {}


class Sched:
    ENG = ("pe", "dve", "act", "pool", "sp")

    def __init__(self, nc, es):
        self.nc = nc
        self.es = es
        self.sems = {}
        for e in self.ENG:
            self.sems[e] = es.enter_context(nc.semaphore("s_" + e))
        self.cnt = {e: 0 for e in self.ENG}
        self.prog = {e: [] for e in self.ENG}
        self.known = {e: {} for e in self.ENG}
        self.dcnt = {}
        self.nbuf = 0

    def buf(self, name=None):
        self.nbuf += 1
        return Buf(name or f"b{self.nbuf}")

    def _waits(self, eng, reads, writes, skip=None):
        need = {}
        for b in reads:
            for k, v in b.w.items():
                if need.get(k, 0) < v:
                    need[k] = v
        for b in writes:
            for k, v in b.w.items():
                if need.get(k, 0) < v:
                    need[k] = v
            for k, v in b.r.items():
                if need.get(k, 0) < v:
                    need[k] = v
        out = []
        kn = self.known[eng]
        for k, v in need.items():
            if k == skip:
                continue
            if k == eng:
                if eng == "pe":
                    continue
                if v > self.cnt[eng]:
                    continue
            if kn.get(k, 0) >= v:
                continue
            kn[k] = v
            out.append((k, v))
        return out

    def _mark(self, tok, reads, writes):
        k, v = tok
        for b in writes:
            b.w = {k: v}
            b.r = {}
        for b in reads:
            if b.r.get(k, 0) < v:
                b.r[k] = v

    def op(self, eng, fn, reads=(), writes=(), signal=True):
        waits = self._waits(eng, reads, writes)
        if signal:
            self.cnt[eng] += 1
            tok = (eng, self.cnt[eng])
        else:
            tok = (eng, self.cnt[eng] + 1)
        self.prog[eng].append(("op", waits, fn, signal))
        self._mark(tok, reads, writes)
        return tok

    def dma(self, q, out, in_, reads=(), writes=(), key=None, **kw):
        if key is None:
            key = "d_" + (writes[0].name if writes else reads[0].name)
        if key not in self.sems:
            self.sems[key] = self.es.enter_context(self.nc.semaphore(key))
            self.dcnt[key] = 0
        waits = self._waits(q, reads, writes, skip=key)
        self.dcnt[key] += 16
        tok = (key, self.dcnt[key])
        self.prog[q].append(("dma", waits, (out, in_, kw), key))
        self._mark(tok, reads, writes)
        return tok

    def barrier(self):
        toks = [(e, self.cnt[e]) for e in self.ENG if self.cnt[e] > 0]
        toks += [(kk, v) for kk, v in self.dcnt.items() if v > 0]
        for e in self.ENG:
            self.wait_all(e, [t for t in toks if t[0] != e])

    def wait_all(self, eng, toks):
        waits = []
        for k, v in toks:
            if self.known[eng].get(k, 0) < v:
                self.known[eng][k] = v
                waits.append((k, v))
        self.prog[eng].append(("wait", waits, None, None))

    def emit(self):
        nc = self.nc
        sems = self.sems
        prog = self.prog

        def replay(ename, e):
            for kind, waits, payload, extra in prog[ename]:
                for k, v in waits:
                    e.wait_ge(sems[k], v)
                if kind == "op":
                    ins = payload(e)
                    if extra:
                        ins.then_inc(sems[ename], 1)
                elif kind == "dma":
                    out, in_, kw = payload
                    e.dma_start(out=out, in_=in_, **kw).then_inc(sems[extra], 16)

        with nc.Block() as block:
            @block.tensor
            def _(e):
                replay("pe", e)

            @block.vector
            def _(e):
                replay("dve", e)

            @block.scalar
            def _(e):
                replay("act", e)

            @block.gpsimd
            def _(e):
                replay("pool", e)

            @block.sync
            def _(e):
                replay("sp", e)


class K:
    def __init__(self, nc, es):
        self.nc = nc
        self.es = es
        self.S = Sched(nc, es)
        self.outs = []

    def sb(self, name, shape, dt):
        t = self.es.enter_context(self.nc.sbuf_tensor("sb_" + name, shape, dt))
        return t, self.S.buf(name)

    def ps(self, name, shape, dt):
        return self.es.enter_context(self.nc.psum_tensor("ps_" + name, shape, dt))

    def mm(self, out, lhsT, rhs, start, stop, reads, writes, signal=True, skip=False):
        if skip:
            return self.S.op("pe", lambda e: e.matmul(out, lhsT=lhsT, rhs=rhs, start=False, stop=False,
                                                      skip_group_check=True), reads, writes, signal)
        return self.S.op("pe", lambda e: e.matmul(out, lhsT=lhsT, rhs=rhs, start=start, stop=stop),
                         reads, writes, signal)

    def tr(self, out, in_, ident, reads, writes, signal=True):
        return self.S.op("pe", lambda e: e.transpose(out, in_, ident), reads, writes, signal)

    def act(self, out, in_, func, reads, writes, bias=None, scale=1.0, accum_out=None):
        def fn(e):
            kw = {}
            if bias is not None:
                kw["bias"] = bias
            if accum_out is not None:
                kw["accum_out"] = accum_out
            return e.activation(out=out, in_=in_, func=func, scale=scale, **kw)
        return self.S.op("act", fn, reads, writes)

    def tt(self, eng, out, in0, in1, op, reads, writes):
        return self.S.op(eng, lambda e: e.tensor_tensor(out=out, in0=in0, in1=in1, op=op), reads, writes)

    def ts(self, eng, out, in0, s1, op0, reads, writes, s2=None, op1=None):
        if op1 is None:
            return self.S.op(eng, lambda e: e.tensor_scalar(out=out, in0=in0, scalar1=s1, scalar2=None, op0=op0),
                             reads, writes)
        return self.S.op(eng, lambda e: e.tensor_scalar(out=out, in0=in0, scalar1=s1, scalar2=s2, op0=op0, op1=op1),
                         reads, writes)

    def stt(self, out, in0, scalar, in1, op0, op1, reads, writes):
        return self.S.op("dve", lambda e: e.scalar_tensor_tensor(out=out, in0=in0, scalar=scalar, in1=in1,
                                                                 op0=op0, op1=op1), reads, writes)

    def cp(self, eng, out, in_, reads, writes):
        if eng == "act":
            return self.S.op("act", lambda e: e.copy(out=out, in_=in_), reads, writes)
        return self.S.op(eng, lambda e: e.tensor_copy(out=out, in_=in_), reads, writes)

    def memset(self, eng, ap, val, writes):
        return self.S.op(eng, lambda e: e.memset(ap, val), (), writes)

    def recip(self, out, in_, reads, writes):
        return self.S.op("dve", lambda e: e.reciprocal(out=out, in_=in_), reads, writes)

    def rstd(self, out, ss, inv_n, eps_t, reads, writes, extra_reads=()):
        self.act(out, ss, AF.Ln, list(reads) + list(extra_reads), writes, bias=eps_t, scale=inv_n)
        self.act(out, out, AF.Exp, writes, writes, scale=-0.5)


def build_phase_a(nc, es, k, x, w_in, cvec, ctab, cmat, ho, nblk=NB, stage=3, npass=1, own_from=0, vmask=None):
    S = k.S
    PB = 128

    ident, b_ident = k.sb("ident", [PB, PB], BF16)
    U, b_U = k.sb("U", [PB, PB], F32)
    SU, b_SU = k.sb("SU", [PB, PB], F32)
    Ub, b_Ub = k.sb("Ub", [PB, PB], BF16)
    ONES, b_ONES = k.sb("ONES", [PB, PB], F32)
    IDF, b_IDF = k.sb("IDF", [PB, PB], F32)
    gpm, b_gpm = k.sb("gpm", [PB, 16], F32)
    cw, b_cw = k.sb("cw", [PB, 6, 4], F32)
    dtb, b_dtb = k.sb("dtb", [PB, 8], F32)
    nA, b_nA = k.sb("nA", [PB, 8], F32)
    gnw, b_gnw = k.sb("gnw", [PB, PB], F32)
    sbw, b_sbw = k.sb("sbw", [PB, PB], F32)
    lamv, b_lamv = k.sb("lamv", [PB, 4, 64], F32)
    eps_t, b_eps = k.sb("eps_t", [PB, 1], F32)
    one_t, b_one = k.sb("one_t", [PB, 1], F32)
    nlam, b_nlam = k.sb("nlam", [PB, 1], F32)
    lt, b_lt = k.sb("lt", [PB, 4], F32)
    ljunk, b_ljunk = k.sb("ljunk", [PB, 64], F32)

    cbufs = [b_ident, b_U, b_SU, b_Ub, b_ONES, b_IDF, b_gpm, b_gnw, b_sbw, b_lamv]
    loads = [(ident[:], cmat["ident"], b_ident), (U[:], cmat["U"], b_U), (SU[:], cmat["SU"], b_SU),
             (Ub[:], cmat["Ub"], b_Ub), (ONES[:], cmat["ONES"], b_ONES), (IDF[:], cmat["IDF"], b_IDF),
             (gpm[:], cvec["gpm"], b_gpm),
             (gnw[:], cvec["gnw"].partition_broadcast(PB), b_gnw),
             (sbw[:], cvec["sbw"].partition_broadcast(PB), b_sbw)]
    for j, nm in enumerate(["lq1", "lk1", "lq2", "lk2"]):
        loads.append((lamv[:, j, :], cvec[nm].partition_broadcast(PB), b_lamv))
    tok = None
    for o, i_, bb in loads:
        tok = S.dma("sp", o, i_, writes=[bb], key="d_const")
    for bb in cbufs:
        bb.w = {tok[0]: tok[1]}
    k.memset("pool", eps_t[:], EPS, [b_eps])
    k.memset("pool", one_t[:], 1.0, [b_one])
    k.ts("dve", sbw[:], sbw[:], 1.0 - LAMBDA_INIT, ALU.mult, [b_sbw], [b_sbw])
    for j in range(2):
        k.tt("dve", ljunk[:], lamv[:, 2 * j, :], lamv[:, 2 * j + 1, :], ALU.mult, [b_lamv], [b_ljunk])
        S.op("dve", lambda e, j=j: e.reduce_sum(out=lt[:, j:j + 1], in_=ljunk[:], axis=AX.X), [b_ljunk], [b_lt])
    k.act(lt[:, 0:2], lt[:, 0:2], AF.Exp, [b_lt], [b_lt])
    k.tt("dve", lt[:, 2:3], lt[:, 1:2], lt[:, 0:1], ALU.subtract, [b_lt], [b_lt])
    k.ts("dve", nlam[:], lt[:, 2:3], -LAMBDA_INIT, ALU.add, [b_lt], [b_nlam])

    w_bufs = [k.sb(f"w_sb{j}", [PB, 16, WC], BF16) for j in range(1)]
    w_sb, b_w = w_bufs[0]
    wcur = [w_sb, b_w]

    x_t, b_xt = k.sb("x_t", [PB, D], F32)
    xs_b, b_xs = k.sb("xs_b", [PB, D], BF16)
    ss, b_ss = k.sb("ss", [PB, 2], F32)
    hT, b_hT = k.sb("hT", [PB, 16, 512], BF16)
    kT = []
    for m in range(2):
        kT.append(k.sb(f"kT{m}", [PB, SEQ], BF16))
    V, b_V = k.sb("V", [PB, 32, 2, 130], BF16)
    b_Vb = [S.buf(f"Vb{j}") for j in range(NB)]
    b_kTb = [[S.buf(f"kTb{m}_{j}") for j in range(NB)] for m in range(2)]
    k.memset("pool", V[:], 1.0, b_Vb)
    if vmask is not None:
        vm, b_vm = k.sb("vm", [PB, 32], F32)
        S.dma("sp", vm[:], vmask, writes=[b_vm])
        for m_ in range(2):
            k.cp("dve", V[:, :, m_, 128], vm[:], [b_vm], b_Vb)
    qTp = [[k.sb(f"qT{p_}_{m}", [PB, 512], BF16) for m in range(2)] for p_ in range(2)]
    cosb, b_cos = k.sb("cosb", [PB, 512], F32)
    sinb, b_sin = k.sb("sinb", [PB, 512], F32)
    st = [k.sb(f"st{t}", [PB, 515], F32) for t in range(6)]
    cacc, b_cacc = k.sb("cacc", [PB, 512], F32)
    cy, b_cy = k.sb("cy", [PB, 512], F32)
    r1, b_r1 = cacc, b_cacc
    r2, b_r2 = cy, b_cy
    csq, b_csq = k.sb("csq", [PB, 512], F32)
    crn, b_crn = k.sb("crn", [PB, 512], F32)
    gqp = [[k.sb(f"gq{p_}_{h}", [PB, 512], F32R) for h in range(2)] for p_ in range(2)]
    gkp = [[k.sb(f"gk{p_}_{h}", [PB, 512], F32R) for h in range(2)] for p_ in range(2)]
    gvp = [[k.sb(f"gv{p_}_{h}", [PB, 512], F32R) for h in range(2)] for p_ in range(2)]
    gt6p = [[gqp[p_][0], gqp[p_][1], gkp[p_][0], gkp[p_][1], gvp[p_][0], gvp[p_][1]] for p_ in range(2)]
    pT = [k.sb(f"pT{j}", [PB, 512], BF16) for j in range(2)]
    zsp = [k.sb(f"zs{p_}", [PB, 4, 256], BF16) for p_ in range(2)]
    ztmp, b_ztmp = csq, b_csq
    bap = [k.sb(f"ba{p_}", [PB, 4, 4], F32) for p_ in range(2)]
    outblk, b_ob = k.sb("outblk", [PB, 4, 512], BF16)
    o1s, b_o1s = k.sb("o1s", [PB, 4, PB], F32)
    dfin, b_dfin = k.sb("dfin", [PB, PB], F32)
    fjunk, b_fjunk = k.sb("fjunk", [PB, PB], F32)
    fs, b_fs = k.sb("fs", [PB, 4], F32)

    beta, b_beta = k.sb("beta", [PB, 8], F32)
    nbeta, b_nbeta = k.sb("nbeta", [PB, 8], F32)
    gt, b_gt = k.sb("gt", [PB, 8], F32)
    gc, b_gc = k.sb("gc", [PB, 8], F32)
    ngc, b_ngc = k.sb("ngc", [PB, 8], F32)
    gtot, b_gtot = k.sb("gtot", [PB, 8], F32)
    egc, b_egc = k.sb("egc", [PB, 8], F32)
    eke, b_eke = k.sb("eke", [PB, 8], F32)
    egl, b_egl = k.sb("egl", [PB, 8], F32)
    GT = []
    for h in range(2):
        T_ = {}
        for nm in ("Gtri", "dm", "decU", "decSU", "ub", "o1g", "og"):
            T_[nm] = k.sb(f"{nm}{h}", [PB, PB], F32)
        T_["fjg"] = T_["dm"]
        for nm in ("PTf", "qkT", "kg", "ke", "vt", "WT", "vnew", "Sr"):
            T_[nm] = k.sb(f"{nm}{h}", [PB, PB], F32R)
        T_["Xb"] = [k.sb(f"Xb{h}_{j}", [PB, PB], F32R) for j in range(2)]
        T_["Yb"] = [k.sb(f"Yb{h}_{j}", [PB, PB], F32R) for j in range(2)]
        T_["fsg"] = k.sb(f"fsg{h}", [PB, 4], F32)
        GT.append(T_)
    b_obg = [S.buf("obg0"), S.buf("obg1")]
    Sf = [k.sb(f"Sf{h}", [PB, PB], F32) for h in range(2)]

    ptr = k.ps("ptr", [PB, 1024], BF16)
    b_ptr0 = S.buf("ptr0")
    b_ptr = [b_ptr0, b_ptr0]
    pin = [k.ps(f"pin{j}", [PB, 512], F32) for j in range(2)]
    b_pin = [S.buf(f"pin{j}") for j in range(2)]
    pst = [k.ps(f"pst{j}", [PB, 512], F32) for j in range(2)]
    b_pst = [S.buf(f"pst{j}") for j in range(2)]
    pacc = [k.ps(f"pacc{j}", [PB, 512], F32) for j in range(2)]
    b_paccb = [S.buf(f"pacc{c}") for c in range(2)]
    b_pacc = [b_paccb[0], b_paccb[0], b_paccb[0], b_paccb[1]]
    pg = k.ps("pg", [PB, 512], F32)
    b_pg = S.buf("pg")
    pgi = [0, 0]

    def pgslot(hd, own=True):
        j = pgi[hd] % 2
        pgi[hd] += 1
        if hd == 0:
            return (pg[:, 0:PB], b_pg) if j == 0 else (pg[:, 128:2 * PB], b_pg)
        if own:
            return (pg[:, 256:256 + PB], b_pg) if j == 0 else (pg[:, 384:384 + PB], b_pg)
        return (pacc[1][:, 256:256 + PB], b_paccb[1]) if j == 0 else (pacc[1][:, 384:384 + PB], b_paccb[1])

    def acc_ap(c):
        if c < 3:
            return pacc[0][:, c * 130:c * 130 + 129]
        return pacc[1][:, 0:129]

    xv = x.rearrange("(t p) d -> t p d", p=PB)

    def load_weights(g):
        w_t, b_t = w_bufs[g % len(w_bufs)]
        wv = w_in[g].rearrange("(kc p) c -> p kc c", p=PB)
        tok = None
        for kc in range(16):
            tok = S.dma("pool", w_t[:, kc, :], wv[:, kc, :], writes=[b_t], key=f"d_w{g % len(w_bufs)}")
        b_t.w = {tok[0]: tok[1]}
        for kc in range(16):
            k.ts("pool" if kc % 2 else "dve", w_t[:, kc, :], w_t[:, kc, :], gpm[:, kc:kc + 1], ALU.mult,
                 [b_t, b_gpm], [b_t])

    loaded = [-1]

    def pass_setup(g):
        if loaded[0] != g:
            load_weights(g)
            loaded[0] = g
        wcur[0], wcur[1] = w_bufs[g % len(w_bufs)]
        S.dma("sp", cw[:], cvec["cw"][g], writes=[b_cw])
        S.dma("sp", dtb[:], cvec["dtb8"][g].partition_broadcast(PB), writes=[b_dtb])
        S.dma("sp", nA[:], cvec["alog8"][g].partition_broadcast(PB), writes=[b_nA])
        k.act(nA[:], nA[:], AF.Exp, [b_nA], [b_nA])
        k.ts("dve", nA[:], nA[:], -1.0, ALU.mult, [b_nA], [b_nA])
        for t in range(6):
            k.memset("pool", st[t][0][:, 0:3], 0.0, [st[t][1]])
        for h in range(2):
            k.memset("pool", Sf[h][0][:], 0.0, [Sf[h][1]])
            k.cp("dve", GT[h]["Sr"][0][:], Sf[h][0][:], [Sf[h][1]], [GT[h]["Sr"][1]])
    pin_i = [0]

    def next_pin():
        j = pin_i[0] % 2
        pin_i[0] += 1
        return pin[j], b_pin[j]

    pst_i = [0]
    pT_i = [0]

    def front_gen(g, blk):
        w_sb, b_w = wcur
        own = blk >= own_from
        par = blk % 2
        qT = qTp[par]
        gt6 = gt6p[par]
        zs, b_zs = zsp[par]
        ba, b_ba = bap[par]
        for tt in range(4):
            T = 4 * blk + tt
            S.dma("sp", x_t[:], xv[T], writes=[b_xt])
            k.act(xs_b[:], x_t[:], AF.Square, [b_xt], [b_xs, b_ss], accum_out=ss[:, 0:1])
            k.rstd(ss[:, 1:2], ss[:, 0:1], 1.0 / D, eps_t[:], [b_ss], [b_ss], extra_reads=[b_eps])
            k.ts("dve", xs_b[:], x_t[:], ss[:, 1:2], ALU.mult, [b_xt, b_ss], [b_xs])
            for half in range(2):
                for j in range(8):
                    kc = half * 8 + j
                    k.tr(ptr[:, j * PB:(j + 1) * PB],
                         xs_b[:, kc * PB:(kc + 1) * PB], ident[:], [b_xs, b_ident], [b_ptr0],
                         signal=(j == 7))
                k.cp("act" if half else "dve", hT[:, half * 8:(half + 1) * 8, tt * PB:(tt + 1) * PB],
                     ptr[:, :].rearrange("p (j c) -> p j c", c=PB), [b_ptr0], [b_hT])
                yield
        S.dma("sp", cosb[:], ctab["cos"][:, blk * 512:(blk + 1) * 512], writes=[b_cos])
        S.dma("sp", sinb[:], ctab["sin"][:, blk * 512:(blk + 1) * 512], writes=[b_sin])

        for gi in range(10 if stage >= 0.6 else (4 if stage >= 0.4 else 0)):
            if not own and (gi in (0, 1) or (gi in (4, 5) and blk != own_from - 1)):
                continue
            pb_, bpb = next_pin()
            for kc in range(16):
                k.mm(pb_[:], w_sb[:, kc, gi * PB:(gi + 1) * PB], hT[:, kc, :], kc == 0, kc == 15,
                     [b_w, b_hT], [bpb], signal=(kc == 15))
                if kc % 4 == 3:
                    yield
            if gi < 4:
                k.tt("dve", r1[:], pb_[:], cosb[:], ALU.mult, [bpb, b_cos], [b_r1])
                for (o0, i0) in ((0, 32), (32, 0), (64, 96), (96, 64)):
                    k.tt("dve", r2[o0:o0 + 32, :], pb_[i0:i0 + 32, :], sinb[i0:i0 + 32, :], ALU.mult,
                         [bpb, b_sin], [b_r2])
                if gi < 2:
                    dst, bdst = qT[gi][0][:], qT[gi][1]
                else:
                    dst, bdst = kT[gi - 2][0][:, blk * 512:(blk + 1) * 512], b_kTb[gi - 2][blk]
                k.tt("pool", dst, r1[:], r2[:], ALU.add, [b_r1, b_r2], [bdst])
            else:
                tix = gi - 4
                stt_, bst = st[tix]
                k.cp("act", stt_[:, 3:515], pb_[:], [bpb], [bst])
                k.ts("dve", cacc[:], stt_[:, 3:515], cw[:, tix, 3:4], ALU.mult, [bst, b_cw], [b_cacc])
                for j in (2, 1, 0):
                    k.stt(cacc[:], stt_[:, j:j + 512], cw[:, tix, j:j + 1], cacc[:], ALU.mult, ALU.add,
                          [bst, b_cw, b_cacc], [b_cacc])
                k.cp("pool", stt_[:, 0:3], stt_[:, 512:515], [bst], [bst])
                dstt, bdst = gt6[tix]
                if tix >= 4:
                    k.act(dstt[:], cacc[:], AF.Silu, [b_cacc], [bdst])
                else:
                    k.act(cy[:], cacc[:], AF.Silu, [b_cacc], [b_cy])
                    k.tt("pool", csq[:], cy[:], cy[:], ALU.mult, [b_cy], [b_csq])
                    pn, bpn = next_pin()
                    k.mm(pn[:], ONES[:], csq[:], True, True, [b_ONES, b_csq], [bpn])
                    k.act(crn[:], pn[:], AF.Ln, [bpn, b_eps], [b_crn], bias=eps_t[:], scale=1.0)
                    k.act(crn[:], crn[:], AF.Exp, [b_crn], [b_crn], scale=-0.5)
                    if tix < 2:
                        k.stt(dstt[:], cy[:], 128.0 ** -0.5, crn[:], ALU.mult, ALU.mult, [b_cy, b_crn], [bdst])
                    else:
                        k.tt("dve", dstt[:], cy[:], crn[:], ALU.mult, [b_cy, b_crn], [bdst])
        for tt in range(4 if stage >= 0.8 else 0):
            T = 4 * blk + tt
            pb_, bpb = next_pin()
            for kc in range(16):
                k.mm(pb_[:, 0:260], hT[:, kc, tt * PB:(tt + 1) * PB], w_sb[:, kc, 1280:1540], kc == 0, kc == 15,
                     [b_w, b_hT], [bpb], signal=(kc == 15))
                if kc % 8 == 7:
                    yield
            import os
            if os.environ.get('DBG_V', '1') == '1':
                k.cp("dve", V[:, T, :, 0:128], pb_[:, 0:256].rearrange("p (h e) -> p h e", e=128), [bpb], [b_Vb[blk]])
            k.cp("dve", ba[:, tt, :], pb_[:, 256:260], [bpb], [b_ba])
            if own:
                pb2, bpb2 = next_pin()
                for kc in range(16):
                    k.mm(pb2[:, 0:256], hT[:, kc, tt * PB:(tt + 1) * PB], w_sb[:, kc, 1540:1796], kc == 0, kc == 15,
                         [b_w, b_hT], [bpb2], signal=(kc == 15))
                k.cp("dve", ztmp[:, 0:256], pb2[:, 0:256], [bpb2], [b_ztmp])
                k.act(zs[:, tt, :], ztmp[:, 0:256], AF.Silu, [b_ztmp], [b_zs])
            yield

    def back_gen(g, blk):
        own = blk >= own_from
        par = blk % 2
        hov = ho[g].rearrange("(t p) c -> p t c", p=PB)
        qT = qTp[par]
        gq, gk, gv = gqp[par], gkp[par], gvp[par]
        zs, b_zs = zsp[par]
        ba, b_ba = bap[par]
        if stage < 1:
            k.outs.append(S.dma("sp", hov[:, 4 * blk:4 * blk + 4, :], outblk[:], reads=[b_ob], key="d_out"))
            return
        bav = ba[:, :, 0:2]
        aav = ba[:, :, 2:4]
        k.act(beta[:].rearrange("p (t h) -> p t h", h=2), bav, AF.Sigmoid, [b_ba], [b_beta])
        k.ts("dve", nbeta[:], beta[:], -1.0, ALU.mult, [b_beta], [b_nbeta])
        k.tt("dve", gt[:].rearrange("p (t h) -> p t h", h=2), aav, dtb[:].rearrange("p (t h) -> p t h", h=2),
             ALU.add, [b_ba, b_dtb], [b_gt])
        k.act(gt[:], gt[:], AF.Exp, [b_gt], [b_gt])
        k.act(gt[:], gt[:], AF.Ln, [b_gt, b_one], [b_gt], bias=one_t[:], scale=1.0)
        k.tt("dve", gt[:], gt[:], nA[:], ALU.mult, [b_gt, b_nA], [b_gt])
        ps1, bps1 = pgslot(0)
        k.mm(ps1[:, 0:8], U[:], gt[:], True, True, [b_U, b_gt], [bps1])
        k.cp("dve", gc[:], ps1[:, 0:8], [bps1], [b_gc])
        ps2, bps2 = pgslot(0)
        k.mm(ps2[:, 0:8], ONES[:], gt[:], True, True, [b_ONES, b_gt], [bps2])
        k.cp("dve", gtot[:], ps2[:, 0:8], [bps2], [b_gtot])
        k.ts("dve", ngc[:], gc[:], -1.0, ALU.mult, [b_gc], [b_ngc])
        k.act(egc[:], gc[:], AF.Exp, [b_gc], [b_egc])
        k.act(egl[:], gtot[:], AF.Exp, [b_gtot], [b_egl])
        k.tt("dve", eke[:], gtot[:], gc[:], ALU.subtract, [b_gtot, b_gc], [b_eke])
        k.act(eke[:], eke[:], AF.Exp, [b_eke], [b_eke])

        def attn_gen(blk=blk):
            nkt = 4 * (blk + 1)
            for h in range(4 if (stage >= 2 and own) else 0):
                m, comp = h // 2, h % 2
                r0 = comp * 64
                S.op("dve", lambda e: e.memset(pacc[0][:, 0:390], 0.0), (), [b_paccb[0]])
                S.op("dve", lambda e: e.memset(pacc[1][:, 0:130], 0.0), (), [b_paccb[1]])
                for kt in range(nkt):
                    j = kt - 4 * blk
                    c0 = max(j, 0) * PB
                    sj = pst_i[0] % 2
                    pst_i[0] += 1
                    k.mm(pst[sj][:, c0:512], kT[m][0][r0:r0 + 64, kt * PB:(kt + 1) * PB],
                         qT[m][0][r0:r0 + 64, c0:512], True, True, [b_kTb[m][kt // 4], qT[m][1]], [b_pst[sj]])
                    pj = pT_i[0] % 2
                    pT_i[0] += 1
                    pTt, bpT = pT[pj]
                    k.act(pTt[:, c0:512], pst[sj][:, c0:512], AF.Exp, [b_pst[sj]], [bpT], scale=0.125)
                    if j >= 0:
                        k.tt("dve", pTt[:, c0:c0 + PB], pTt[:, c0:c0 + PB], Ub[:], ALU.mult, [bpT, b_Ub], [bpT])
                    cs = list(range(max(j, 0), 4))
                    for c in cs:
                        k.mm(acc_ap(c), pTt[:, c * PB:(c + 1) * PB], V[:, kt, m, 0:129], False, False,
                             [bpT, b_Vb[kt // 4]], [b_pacc[c]], signal=(c == cs[-1]), skip=True)
                    yield
                for c in range(4):
                    a = acc_ap(c)
                    k.recip(fs[:, 0:1], a[:, 128:129], [b_pacc[c]], [b_fs])
                    if comp == 0:
                        k.ts("dve", o1s[:, c, :], a[:, 0:128], fs[:, 0:1], ALU.mult, [b_pacc[c], b_fs], [b_o1s])
                    else:
                        k.tt("dve", fs[:, 1:2], fs[:, 0:1], nlam[:], ALU.mult, [b_fs, b_nlam], [b_fs])
                        k.stt(dfin[:], a[:, 0:128], fs[:, 1:2], o1s[:, c, :], ALU.mult, ALU.add,
                              [b_pacc[c], b_fs, b_o1s], [b_dfin])
                        k.act(fjunk[:], dfin[:], AF.Square, [b_dfin], [b_fjunk, b_fs], accum_out=fs[:, 2:3])
                        k.rstd(fs[:, 3:4], fs[:, 2:3], 1.0 / 128, eps_t[:], [b_fs], [b_fs], extra_reads=[b_eps])
                        k.stt(outblk[:, c, m * PB:(m + 1) * PB], dfin[:], fs[:, 3:4], sbw[:], ALU.mult, ALU.mult,
                              [b_dfin, b_fs, b_sbw], [b_ob])
                    yield

        def gdn_gen(hd, blk=blk):
            T_ = GT[hd]
            Gtri, b_Gtri = T_["Gtri"]; dm, b_dm = T_["dm"]; decU, b_decU = T_["decU"]; decSU, b_decSU = T_["decSU"]
            Xb = T_["Xb"]; Yb = T_["Yb"]; PTf, b_PTf = T_["PTf"]; qkT, b_qkT = T_["qkT"]; kg, b_kg = T_["kg"]
            ke, b_ke = T_["ke"]; vt, b_vt = T_["vt"]; ub, b_ub = T_["ub"]; WT, b_WT = T_["WT"]
            vnew, b_vnew = T_["vnew"]; o1g, b_o1g = T_["o1g"]; og, b_og = T_["og"]; Sr, bSr = T_["Sr"]
            fsg, b_fsg = T_["fsg"]; fjg, b_fjg = T_["fjg"]
            gqt, bgq = gq[hd]
            gkt, bgk = gk[hd]
            gvt, bgv = gv[hd]
            Sft, bSf = Sf[hd]
            for cc in range(4 if stage >= 3 else 0):
                csl = slice(cc * PB, (cc + 1) * PB)
                q8 = cc * 2 + hd
                sc = lambda t, q8=q8: t[:, q8:q8 + 1]
                k.ts("dve", Gtri[:], U[:], sc(gt), ALU.mult, [b_U, b_gt], [b_Gtri])
                p_a, bp_a = pgslot(hd, own)
                k.mm(p_a, ONES[:], Gtri[:], True, True, [b_ONES, b_Gtri], [bp_a])
                k.ts("dve", dm[:], p_a, sc(ngc), ALU.add, [bp_a, b_ngc], [b_dm], s2=0.0, op1=ALU.min)
                k.act(dm[:], dm[:], AF.Exp, [b_dm], [b_dm])
                k.tt("dve", decU[:], dm[:], U[:], ALU.mult, [b_dm, b_U], [b_decU])
                k.tt("dve", decSU[:], dm[:], SU[:], ALU.mult, [b_dm, b_SU], [b_decSU])
                yield
                p_b, bp_b = pgslot(hd, own)
                k.mm(p_b, gkt[:, csl], gkt[:, csl], True, True, [bgk], [bp_b])
                k.stt(PTf[:], p_b, sc(nbeta), decSU[:], ALU.mult, ALU.mult, [bp_b, b_nbeta, b_decSU], [b_PTf])
                k.cp("dve", Yb[0][0][:], PTf[:], [b_PTf], [Yb[0][1]])
                k.tt("dve", PTf[:], PTf[:], IDF[:], ALU.add, [b_PTf, b_IDF], [b_PTf])
                yield
                if own:
                    p_c, bp_c = pgslot(hd, own)
                    k.mm(p_c, gkt[:, csl], gqt[:, csl], True, True, [bgk, bgq], [bp_c])
                    k.tt("dve", qkT[:], p_c, decU[:], ALU.mult, [bp_c, b_decU], [b_qkT])
                    yield
                p_t, bp_t = pgslot(hd, own)
                k.tr(p_t, Yb[0][0][:].bitcast(F32), IDF[:], [Yb[0][1], b_IDF], [bp_t])
                k.cp("act", Xb[0][0][:], p_t, [bp_t], [Xb[0][1]])
                yield
                p_k, bp_k = pgslot(hd, own)
                k.tr(p_k, gkt[:, csl].bitcast(F32), IDF[:], [bgk, b_IDF], [bp_k])
                k.ts("dve", kg[:], p_k, sc(egc), ALU.mult, [bp_k, b_egc], [b_kg])
                k.ts("dve", ke[:], p_k, sc(eke), ALU.mult, [bp_k, b_eke], [b_ke])
                yield
                p_v, bp_v = pgslot(hd, own)
                k.tr(p_v, gvt[:, csl].bitcast(F32), IDF[:], [bgv, b_IDF], [bp_v])
                k.cp("act", vt[:], p_v, [bp_v], [b_vt])
                yield
                cur = 0
                for lvl in range(6):
                    nxt = 1 - cur
                    Xc, bXc = Xb[cur]
                    Yc, bYc = Yb[cur]
                    Xn, bXn = Xb[nxt]
                    Yn, bYn = Yb[nxt]
                    p_x, bp_x = pgslot(hd, own)
                    k.mm(p_x, Yc[:], Xc[:], True, True, [bYc, bXc], [bp_x])
                    k.cp("act", Xn[:], p_x, [bp_x], [bXn])
                    yield
                    if lvl < 5:
                        p_y, bp_y = pgslot(hd, own)
                        k.mm(p_y, Xc[:], Yc[:], True, True, [bYc, bXc], [bp_y])
                        k.cp("dve", Yn[:], p_y, [bp_y], [bYn])
                        yield
                    p_p, bp_p = pgslot(hd, own)
                    k.mm(p_p, Xn[:], PTf[:], True, True, [bXn, b_PTf], [bp_p])
                    k.tt("dve", PTf[:], p_p, PTf[:], ALU.add, [bp_p, b_PTf], [b_PTf])
                    yield
                    cur = nxt
                p_u, bp_u = pgslot(hd, own)
                k.mm(p_u, PTf[:], vt[:], True, True, [b_PTf, b_vt], [bp_u])
                k.ts("dve", ub[:], p_u, sc(beta), ALU.mult, [bp_u, b_beta], [b_ub])
                yield
                p_w, bp_w = pgslot(hd, own)
                k.mm(p_w, kg[:], PTf[:], True, True, [b_kg, b_PTf], [bp_w])
                k.cp("act", WT[:], p_w, [bp_w], [b_WT])
                yield
                p_i, bp_i = pgslot(hd, own)
                k.mm(p_i, WT[:], Sr[:], True, True, [b_WT, bSr], [bp_i])
                k.stt(vnew[:], p_i, sc(nbeta), ub[:], ALU.mult, ALU.add, [bp_i, b_nbeta, b_ub], [b_vnew])
                yield
                if own:
                    p_o1, bp_o1 = pgslot(hd, own)
                    k.mm(p_o1, gqt[:, csl], Sr[:], True, True, [bgq, bSr], [bp_o1])
                    k.ts("dve", o1g[:], p_o1, sc(egc), ALU.mult, [bp_o1, b_egc], [b_o1g])
                    yield
                    p_o2, bp_o2 = pgslot(hd, own)
                    k.mm(p_o2, qkT[:], vnew[:], True, True, [b_qkT, b_vnew], [bp_o2])
                    k.tt("dve", og[:], p_o2, o1g[:], ALU.add, [bp_o2, b_o1g], [b_og])
                    yield
                p_s, bp_s = pgslot(hd, own)
                k.mm(p_s, ke[:], vnew[:], True, True, [b_ke, b_vnew], [bp_s])
                k.stt(Sr[:], Sft[:], sc(egl), p_s, ALU.mult, ALU.add, [bSf, b_egl, bp_s], [bSr])
                k.stt(Sft[:], Sft[:], sc(egl), p_s, ALU.mult, ALU.add, [bSf, b_egl, bp_s], [bSf])
                yield
                if not own:
                    continue
                k.act(fjg[:], og[:], AF.Square, [b_og], [b_fjg, b_fsg], accum_out=fsg[:, 2:3])
                k.rstd(fsg[:, 3:4], fsg[:, 2:3], 1.0 / 128, eps_t[:], [b_fsg], [b_fsg], extra_reads=[b_eps])
                k.stt(og[:], og[:], fsg[:, 3:4], gnw[:], ALU.mult, ALU.mult, [b_og, b_fsg, b_gnw], [b_og])
                k.tt("dve", outblk[:, cc, 256 + hd * PB:256 + (hd + 1) * PB], og[:],
                     zs[:, cc, hd * PB:(hd + 1) * PB], ALU.mult, [b_og, b_zs], [b_obg[hd]])
                yield

        gens = [attn_gen(), gdn_gen(0), gdn_gen(1)]
        while gens:
            for gn in list(gens):
                try:
                    next(gn)
                except StopIteration:
                    gens.remove(gn)
            yield
        if own:
            ob0 = 4 * (blk - own_from)
            k.outs.append(S.dma("sp", hov[:, ob0:ob0 + 4, :], outblk[:], reads=[b_ob, b_obg[0], b_obg[1]], key="d_out"))
    def run_all(gn):
        for _ in gn:
            pass

    RATIO = 2
    for g in range(npass):
        pass_setup(g)
        run_all(front_gen(g, 0))
        for blk in range(nblk):
            fr = front_gen(g, blk + 1) if blk + 1 < nblk else None
            bk = back_gen(g, blk)
            cnt_ = 0
            while bk is not None or fr is not None:
                if bk is not None:
                    try:
                        next(bk)
                    except StopIteration:
                        bk = None
                cnt_ += 1
                if fr is not None and (bk is None or cnt_ % RATIO == 0):
                    try:
                        next(fr)
                    except StopIteration:
                        fr = None
                        if blk + 1 == nblk - 1 and g + 1 < npass and own_from > 0:
                            load_weights(g + 1)
                            loaded[0] = g + 1
    if nblk == 0:
        pass_setup(0)
        hov = ho[0].rearrange("(t p) c -> p t c", p=PB)
        k.cp("dve", outblk[:, 0, :], w_sb[:, 0, 0:512], [b_w], [b_ob])
        k.cp("dve", outblk[:, 1, :], w_sb[:, 15, 0:512], [b_w], [b_ob])
        k.outs.append(S.dma("sp", hov[:, 0:4, :], outblk[:], reads=[b_ob], key="d_out"))


def _consts():
    bf = ml_dtypes.bfloat16
    r = np.arange(128)
    U = (r[None, :] >= r[:, None]).astype(np.float32)
    SU = (r[None, :] > r[:, None]).astype(np.float32)
    inv_freq = (10000.0 ** (-np.arange(0, 64, 2, dtype=np.float32) / np.float32(64))).astype(np.float32)
    ang = np.arange(SEQ, dtype=np.float32)[None, :] * inv_freq[:, None]
    fi = (r % 64) % 32
    cos = np.cos(ang.astype(np.float64)).astype(np.float32)[fi]
    sin = np.sin(ang.astype(np.float64)).astype(np.float32)[fi]
    sgn = np.where((r % 64) < 32, 1.0, -1.0).astype(np.float32)[:, None]
    return {"ident": np.eye(128).astype(bf), "U": U, "SU": SU, "Ub": U.astype(bf),
            "ONES": np.ones((128, 128), np.float32), "IDF": np.eye(128, dtype=np.float32),
            "cos": cos, "sin": (sin * sgn).astype(np.float32)}


def _phase_a_program(nblk=NB, stage=3):
    nc = bass.Bass("TRN2", target_bir_lowering=False)
    def din(name, shape, dt=F32):
        return nc.dram_tensor(name, shape, dt, kind="ExternalInput").ap()
    x = din("x", [SEQ, D])
    w_in = din("w_in", [1, D, WC])
    cvec = {"gpm": din("gpm", [128, 16]), "cw": din("cw", [1, 128, 6, 4]), "dtb8": din("dtb8", [1, 8]),
            "alog8": din("alog8", [1, 8]), "gnw": din("gnw", [128]), "sbw": din("sbw", [128]),
            "lq1": din("lq1", [64]), "lk1": din("lk1", [64]), "lq2": din("lq2", [64]), "lk2": din("lk2", [64])}
    ctab = {"cos": din("cos", [128, SEQ]), "sin": din("sin", [128, SEQ])}
    cmat = {"ident": din("ident", [128, 128], BF16), "U": din("U", [128, 128]), "SU": din("SU", [128, 128]),
            "Ub": din("Ub", [128, 128], BF16), "ONES": din("ONES", [128, 128]), "IDF": din("IDF", [128, 128])}
    ho = nc.dram_tensor("ho", [1, SEQ, 512], BF16, kind="ExternalOutput").ap()
    with contextlib.ExitStack() as es:
        k = K(nc, es)
        build_phase_a(nc, es, k, x, w_in, cvec, ctab, cmat, ho, nblk, stage)
        k.S.wait_all("sp", k.outs)
        k.S.emit()
    return nc


def _phase_a_inputs(inp):
    C = _consts()
    w = inp["w_in"][0]
    maps = []
    for r in range(8):
        b, g = r // 4, r % 4
        cols = []
        c256 = np.arange(256 * g, 256 * g + 256)
        cols += list(c256)
        cols += list(1024 + c256)
        cols += list(3072 + c256)
        cols += list(4096 + c256)
        cols += list(5120 + c256)
        cols += list(2048 + c256)
        cols += [7168 + 2 * g, 7169 + 2 * g, 7176 + 2 * g, 7177 + 2 * g]
        cols += list(6144 + c256)
        cols = np.asarray(cols)
        cwc = inp["conv_w"][0]
        ch = np.concatenate([c256, 1024 + c256, 2048 + c256])
        cw = np.ascontiguousarray(cwc[:, ch].reshape(4, 6, 128).transpose(2, 1, 0))
        hs = [2 * g, 2 * g + 1]
        m = {"x": np.ascontiguousarray(inp["x"][b]),
             "w_in": np.ascontiguousarray(w[:, cols])[None],
             "gpm": np.ascontiguousarray(inp["norm_pre_mix"][0].reshape(16, 128).T),
             "cw": cw[None],
             "dtb8": np.ascontiguousarray(np.tile(inp["dt_bias"][0][hs], 4))[None],
             "alog8": np.ascontiguousarray(np.tile(inp["a_log"][0][hs], 4))[None],
             "gnw": np.ascontiguousarray(inp["gdn_norm_w"][0]),
             "sbw": np.ascontiguousarray(inp["da_subln_w"][0]),
             "lq1": np.ascontiguousarray(inp["lambda_q1"][0]), "lk1": np.ascontiguousarray(inp["lambda_k1"][0]),
             "lq2": np.ascontiguousarray(inp["lambda_q2"][0]), "lk2": np.ascontiguousarray(inp["lambda_k2"][0])}
        m.update(C)
        maps.append(m)
    return maps


def run_phase_a(inp, nblk=NB, stage=3, ncores=8):
    nc = _phase_a_program(nblk, stage)
    maps = _phase_a_inputs(inp)
    import os
    tr = os.environ.get("KTRACE", "0") == "1"
    res = run_bass_kernel_spmd(nc, maps[:ncores], core_ids=list(range(ncores)), trace=tr)
    if tr:
        print("PHASE_A exec_time_ns", res.exec_time_ns)
    return [r["ho"][0] for r in res.results]


NT = 8


def build_phase_b(nc, es, k, mixin, xs, w_out, w_up, w_down, gpost, gpm2, gpost2, identd, out, ho4=None, qreg=None):
    S = k.S
    PB = 128
    ident, b_ident = k.sb("identB", [PB, PB], BF16)
    eps_t, b_eps = k.sb("epsB", [PB, 1], F32)
    g2, b_g2 = k.sb("g2", [PB, 16], F32)
    S.dma("sp", ident[:], identd, writes=[b_ident])
    S.dma("sp", g2[:], gpm2, writes=[b_g2])
    k.memset("pool", eps_t[:], EPS, [b_eps])
    Y, b_Yw = k.sb("Y", [PB, NT, D], F32)
    b_Y = [S.buf(f"Y{t}") for t in range(NT)]
    mT, b_mT = k.sb("mT", [PB, 16, NT * PB], BF16)
    wo = [k.sb(f"wo{j}", [PB, 16, 256], BF16) for j in range(2)]
    wu = [k.sb(f"wu{j}", [PB, 16, PB], BF16) for j in range(2)]
    wd = [k.sb(f"wd{j}", [PB, 8, 512], BF16) for j in range(2)]
    hidT, b_hid = k.sb("hidT", [PB, 8, NT * PB], BF16)
    x_t, b_xt = k.sb("xB", [PB, D], F32)
    xs_b, b_xs = k.sb("xsB", [PB, D], BF16)
    gB, b_gB = k.sb("gB", [PB, D], F32)
    rtmp, b_rtmp = k.sb("rtmp", [PB, 512], F32)
    ss, b_ss = k.sb("ssB", [PB, 2], F32)
    ptr = k.ps("ptrB", [PB, 1024], BF16)
    b_ptr = S.buf("ptrB")
    pin = [k.ps(f"pinB{j}", [PB, 512], F32) for j in range(2)]
    b_pin = [S.buf(f"pinB{j}") for j in range(2)]
    pst = [k.ps(f"pstB{j}", [PB, 512], F32) for j in range(2)]
    b_pst = [S.buf(f"pstB{j}") for j in range(2)]
    b_od = [S.buf(f"od{t}") for t in range(NT)]
    mv = mixin.rearrange("(t p) c -> t p c", p=PB) if mixin is not None else None
    xv = xs.rearrange("(t p) c -> t p c", p=PB)
    ov = out.rearrange("(t p) c -> t p c", p=PB)
    pin_i = [0]
    pst_i = [0]

    S.dma("sp", gB[:], gpost.partition_broadcast(PB), writes=[b_gB])
    for tt in range(NT):
        if ho4 is None:
            S.dma("sp", xs_b[:], mv[tt], writes=[b_xs])
        else:
            hv = ho4.rearrange("g (t p) c -> t p g c", p=PB)
            S.dma("sp", xs_b[:].rearrange("p (g c) -> p g c", g=4), hv[tt], writes=[b_xs])
        for half in range(2):
            for j in range(8):
                kc = half * 8 + j
                k.tr(ptr[:, j * PB:(j + 1) * PB], xs_b[:, kc * PB:(kc + 1) * PB], ident[:], [b_xs, b_ident], [b_ptr],
                     signal=(j == 7))
            k.cp("act" if half else "dve", mT[:, half * 8:(half + 1) * 8, tt * PB:(tt + 1) * PB],
                 ptr[:, :].rearrange("p (j c) -> p j c", c=PB), [b_ptr], [b_mT])
    wov = w_out.rearrange("(kc p) c -> p kc c", p=PB)
    for cg in range(8):
        wt, bw = wo[cg % 2]
        S.dma("pool", wt[:], wov[:, :, cg * 256:(cg + 1) * 256], writes=[bw])
        for tt in range(NT):
            j = pin_i[0] % 2
            pin_i[0] += 1
            for kc in range(16):
                k.mm(pin[j][:, 0:256], mT[:, kc, tt * PB:(tt + 1) * PB], wt[:, kc, :], kc == 0, kc == 15,
                     [b_mT, bw], [b_pin[j]], signal=(kc == 15))
            k.cp("act", Y[:, tt, cg * 256:(cg + 1) * 256], pin[j][:, 0:256], [b_pin[j]], [b_Y[tt]])
    for tt in range(NT):
        k.act(xs_b[:], Y[:, tt, :], AF.Square, [b_Y[tt]], [b_xs, b_ss], accum_out=ss[:, 0:1])
        k.rstd(ss[:, 1:2], ss[:, 0:1], 1.0 / D, eps_t[:], [b_ss], [b_ss], extra_reads=[b_eps])
        S.dma("sp", x_t[:], xv[tt], writes=[b_xt])
        k.stt(Y[:, tt, :], Y[:, tt, :], ss[:, 1:2], gB[:], ALU.mult, ALU.mult, [b_Y[tt], b_ss, b_gB], [b_Y[tt]])
        k.tt("pool", x_t[:], Y[:, tt, :], x_t[:], ALU.add, [b_Y[tt], b_xt], [b_xt])
        S.dma("sp", ov[tt], x_t[:], reads=[b_xt], writes=[b_od[tt]], key="d_x1st")
        k.act(xs_b[:], x_t[:], AF.Square, [b_xt], [b_xs, b_ss], accum_out=ss[:, 0:1])
        k.rstd(ss[:, 1:2], ss[:, 0:1], 1.0 / D, eps_t[:], [b_ss], [b_ss], extra_reads=[b_eps])
        k.ts("dve", xs_b[:], x_t[:], ss[:, 1:2], ALU.mult, [b_xt, b_ss], [b_xs])
        for half in range(2):
            for j in range(8):
                kc = half * 8 + j
                k.tr(ptr[:, j * PB:(j + 1) * PB], xs_b[:, kc * PB:(kc + 1) * PB], ident[:], [b_xs, b_ident], [b_ptr],
                     signal=(j == 7))
            for j in range(8):
                kc = half * 8 + j
                k.ts("dve", mT[:, kc, tt * PB:(tt + 1) * PB], ptr[:, j * PB:(j + 1) * PB], g2[:, kc:kc + 1], ALU.mult,
                     [b_ptr, b_g2], [b_mT])
    S.dma("sp", gB[:], gpost2.partition_broadcast(PB), writes=[b_gB])
    wuv = w_up.rearrange("(kc p) f -> p kc f", p=PB)
    for fg in range(8):
        for fc in range(8):
            F = fg * 8 + fc
            wt, bw = wu[F % 2]
            S.dma("pool", wt[:], wuv[:, :, F * PB:(F + 1) * PB], writes=[bw])
            for tb in range(2):
                j = pst_i[0] % 2
                pst_i[0] += 1
                for kc in range(16):
                    k.mm(pst[j][:], wt[:, kc, :], mT[:, kc, tb * 512:(tb + 1) * 512], kc == 0, kc == 15,
                         [bw, b_mT], [b_pst[j]], signal=(kc == 15))
                k.ts("dve", rtmp[:], pst[j][:], 0.0, ALU.max, [b_pst[j]], [b_rtmp])
                k.tt("pool", hidT[:, fc, tb * 512:(tb + 1) * 512], rtmp[:], rtmp[:], ALU.mult, [b_rtmp], [b_hid])
        wdv = w_down[fg * 1024:(fg + 1) * 1024, :].rearrange("(fc p) c -> p fc c", p=PB)
        for cg in range(4):
            wt, bw = wd[cg % 2]
            S.dma("pool", wt[:], wdv[:, :, cg * 512:(cg + 1) * 512], writes=[bw])
            for tt in range(NT):
                j = pin_i[0] % 2
                pin_i[0] += 1
                for fc in range(8):
                    k.mm(pin[j][:], hidT[:, fc, tt * PB:(tt + 1) * PB], wt[:, fc, :], fc == 0, fc == 7,
                         [b_hid, bw], [b_pin[j]], signal=(fc == 7))
                ysl = Y[:, tt, cg * 512:(cg + 1) * 512]
                if fg == 0:
                    k.cp("act", ysl, pin[j][:], [b_pin[j]], [b_Y[tt]])
                else:
                    k.tt("dve", ysl, pin[j][:], ysl, ALU.add, [b_pin[j], b_Y[tt]], [b_Y[tt]])
    for tt in range(NT):
        k.act(xs_b[:], Y[:, tt, :], AF.Square, [b_Y[tt]], [b_xs, b_ss], accum_out=ss[:, 0:1])
        k.rstd(ss[:, 1:2], ss[:, 0:1], 1.0 / D, eps_t[:], [b_ss], [b_ss], extra_reads=[b_eps])
        S.dma("sp", x_t[:], ov[tt], reads=[b_od[tt]], writes=[b_xt])
        k.stt(Y[:, tt, :], Y[:, tt, :], ss[:, 1:2], gB[:], ALU.mult, ALU.mult, [b_Y[tt], b_ss, b_gB], [b_Y[tt]])
        k.tt("pool", x_t[:], Y[:, tt, :], x_t[:], ALU.add, [b_Y[tt], b_xt], [b_xt])
        k.outs.append(S.dma("sp", ov[tt], x_t[:], reads=[b_xt], writes=[b_od[tt]], key="d_fin"))


def _phase_b_program():
    nc = bass.Bass("TRN2", target_bir_lowering=False)
    def din(name, shape, dt=F32):
        return nc.dram_tensor(name, shape, dt, kind="ExternalInput").ap()
    mixin = din("mixin", [NT * 128, D], BF16)
    xs = din("xs", [NT * 128, D])
    w_out = din("w_out", [D, D])
    w_up = din("w_up", [D, 4 * D])
    w_down = din("w_down", [4 * D, D])
    gpost = din("gpost", [D])
    gpm2 = din("gpm2", [128, 16])
    gpost2 = din("gpost2", [D])
    identd = din("ident", [128, 128], BF16)
    out = nc.dram_tensor("out", [NT * 128, D], F32, kind="ExternalOutput").ap()
    with contextlib.ExitStack() as es:
        k = K(nc, es)
        build_phase_b(nc, es, k, mixin, xs, w_out, w_up, w_down, gpost, gpm2, gpost2, identd, out)
        k.S.wait_all("sp", k.outs)
        k.S.emit()
    return nc


def _phase_b_inputs(inp, hos):
    bf = ml_dtypes.bfloat16
    maps = []
    ident = np.eye(128).astype(bf)
    for r in range(8):
        b, q = r // 4, r % 4
        rows = slice(q * 1024, (q + 1) * 1024)
        mix = np.concatenate([hos[4 * b + g][rows, 0:256] for g in range(4)] +
                             [hos[4 * b + g][rows, 256:512] for g in range(4)], axis=1)
        maps.append({"mixin": np.ascontiguousarray(mix),
                     "xs": np.ascontiguousarray(inp["x"][b, rows]),
                     "w_out": np.ascontiguousarray(inp["w_out"][0]),
                     "w_up": np.ascontiguousarray(inp["w_up"][0]),
                     "w_down": np.ascontiguousarray(inp["w_down"][0]),
                     "gpost": np.ascontiguousarray(inp["norm_post_mix"][0]),
                     "gpm2": np.ascontiguousarray(inp["norm_pre_mlp"][0].reshape(16, 128).T),
                     "gpost2": np.ascontiguousarray(inp["norm_post_mlp"][0]),
                     "ident": ident})
    return maps


def kernel(**inp):
    inp = {k_: np.asarray(v) for k_, v in inp.items()}
    nc = _fused_program()
    res = run_bass_kernel_spmd(nc, _fused_inputs(inp), core_ids=list(range(8)))
    out = np.empty((2, SEQ, D), np.float32)
    for r in range(8):
        b, q = r // 4, r % 4
        out[b, q * 1024:(q + 1) * 1024] = res.results[r]["out"]
    return out


def _fused_program(nblk=NB):
    nc = bass.Bass("TRN2", target_bir_lowering=False)
    def din(name, shape, dt=F32):
        return nc.dram_tensor(name, shape, dt, kind="ExternalInput").ap()
    x = din("x", [SEQ, D])
    w_in = din("w_in", [4, D, WC])
    cvec = {"gpm": din("gpm", [128, 16]), "cw": din("cw", [4, 128, 6, 4]), "dtb8": din("dtb8", [4, 8]),
            "alog8": din("alog8", [4, 8]), "gnw": din("gnw", [128]), "sbw": din("sbw", [128]),
            "lq1": din("lq1", [64]), "lk1": din("lk1", [64]), "lq2": din("lq2", [64]), "lk2": din("lk2", [64])}
    ctab = {"cos": din("cos", [128, SEQ]), "sin": din("sin", [128, SEQ])}
    cmat = {"ident": din("ident", [128, 128], BF16), "U": din("U", [128, 128]), "SU": din("SU", [128, 128]),
            "Ub": din("Ub", [128, 128], BF16), "ONES": din("ONES", [128, 128]), "IDF": din("IDF", [128, 128])}
    xs = din("xs", [NT * 128, D])
    w_out = din("w_out", [D, D])
    w_up = din("w_up", [D, 4 * D])
    w_down = din("w_down", [4 * D, D])
    gpost = din("gpost", [D])
    gpm2 = din("gpm2", [128, 16])
    gpost2 = din("gpost2", [D])
    out = nc.dram_tensor("out", [NT * 128, D], F32, kind="ExternalOutput").ap()
    vmask = din("vmask", [128, 32])
    ho4 = nc.dram_tensor("ho4", [4, NT * 128, 512], BF16).ap()
    with contextlib.ExitStack() as top:
        k = K(nc, top)
        with contextlib.ExitStack() as esA:
            k.es = esA
            build_phase_a(nc, esA, k, x, w_in, cvec, ctab, cmat, ho4, nblk=nblk, npass=4, own_from=nblk - 2,
                          vmask=vmask)
        k.S.barrier()
        k.outs = []
        with contextlib.ExitStack() as esB:
            k.es = esB
            build_phase_b(nc, esB, k, None, xs, w_out, w_up, w_down, gpost, gpm2, gpost2, cmat["ident"], out,
                          ho4=ho4)
        k.S.wait_all("sp", k.outs)
        k.S.emit()
    return nc


def _fused_inputs(inp):
    a_maps = _phase_a_inputs(inp)
    maps = []
    perm = np.asarray([half * 1024 + g * 256 + j for g in range(4) for half in range(2) for j in range(256)])
    w_out_p = np.ascontiguousarray(inp["w_out"][0][perm])
    for r in range(8):
        b, q = r // 4, r % 4
        m = dict(a_maps[4 * b])
        for nm in ("w_in", "cw", "dtb8", "alog8"):
            m[nm] = np.ascontiguousarray(np.concatenate([a_maps[4 * b + g][nm] for g in range(4)], axis=0))
        rows = slice(q * 1024, (q + 1) * 1024)
        pad = (3 - q) * 1024
        xsh = np.zeros((SEQ, D), np.float32)
        xsh[pad:] = inp["x"][b, :(q + 1) * 1024]
        m["x"] = xsh
        for nm in ("cos", "sin"):
            tab = np.zeros((128, SEQ), np.float32)
            tab[:, pad:] = a_maps[0][nm][:, :SEQ - pad]
            m[nm] = tab
        vmk = np.zeros((128, 32), np.float32)
        vmk[:, pad // 128:] = 1.0
        m["vmask"] = vmk
        m.update({"xs": np.ascontiguousarray(inp["x"][b, rows]),
                  "w_out": w_out_p,
                  "w_up": np.ascontiguousarray(inp["w_up"][0]),
                  "w_down": np.ascontiguousarray(inp["w_down"][0]),
                  "gpost": np.ascontiguousarray(inp["norm_post_mix"][0]),
                  "gpm2": np.ascontiguousarray(inp["norm_pre_mlp"][0].reshape(16, 128).T),
                  "gpost2": np.ascontiguousarray(inp["norm_post_mlp"][0])})
        maps.append(m)
    return maps


def kernel_unfused(**inp):
    inp = {k_: np.asarray(v) for k_, v in inp.items()}
    hos = run_phase_a(inp)
    ncb = _phase_b_program()
    res = run_bass_kernel_spmd(ncb, _phase_b_inputs(inp, hos), core_ids=list(range(8)))
    out = np.empty((2, SEQ, D), np.float32)
    for r in range(8):
        b, q = r // 4, r % 4
        out[b, q * 1024:(q + 1) * 1024] = res.results[r]["out"]
    return out
```

```python
import contextlib
import numpy as np
import ml_dtypes
import concourse.bass as bass
import concourse.mybir as mybir
from concourse.bass_utils import run_bass_kernel_spmd

F32 = mybir.dt.float32
BF16 = mybir.dt.bfloat16
F32R = mybir.dt.float32r
ALU = mybir.AluOpType
AF = mybir.ActivationFunctionType
AX = mybir.AxisListType

D = 2048
SEQ = 4096
NB = 8
EPS = 1e-6
LAMBDA_INIT = 0.2
WC = 1796


class Buf:
    __slots__ = ("name", "w", "r")

    def __init__(self, name):
        self.name = name
        self.w = {}
        self.r = {}


class Sched:
    ENG = ("pe", "dve", "act", "pool", "sp")

    def __init__(self, nc, es):
        self.nc = nc
        self.es = es
        self.sems = {}
        for e in self.ENG:
            self.sems[e] = es.enter_context(nc.semaphore("s_" + e))
        self.cnt = {e: 0 for e in self.ENG}
        self.prog = {e: [] for e in self.ENG}
        self.known = {e: {} for e in self.ENG}
        self.dcnt = {}
        self.nbuf = 0

    def buf(self, name=None):
        self.nbuf += 1
        return Buf(name or f"b{self.nbuf}")

    def _waits(self, eng, reads, writes, skip=None):
        need = {}
        for b in reads:
            for k, v in b.w.items():
                if need.get(k, 0) < v:
                    need[k] = v
        for b in writes:
            for k, v in b.w.items():
                if need.get(k, 0) < v:
                    need[k] = v
            for k, v in b.r.items():
                if need.get(k, 0) < v:
                    need[k] = v
        out = []
        kn = self.known[eng]
        for k, v in need.items():
            if k == skip:
                continue
            if k == eng:
                if eng == "pe":
                    continue
                if v > self.cnt[eng]:
                    continue
            if kn.get(k, 0) >= v:
                continue
            kn[k] = v
            out.append((k, v))
        return out

    def _mark(self, tok, reads, writes):
        k, v = tok
        for b in writes:
            b.w = {k: v}
            b.r = {}
        for b in reads:
            if b.r.get(k, 0) < v:
                b.r[k] = v

    def op(self, eng, fn, reads=(), writes=(), signal=True):
        waits = self._waits(eng, reads, writes)
        if signal:
            self.cnt[eng] += 1
            tok = (eng, self.cnt[eng])
        else:
            tok = (eng, self.cnt[eng] + 1)
        self.prog[eng].append(("op", waits, fn, signal))
        self._mark(tok, reads, writes)
        return tok

    def dma(self, q, out, in_, reads=(), writes=(), key=None, **kw):
        if key is None:
            key = "d_" + (writes[0].name if writes else reads[0].name)
        if key not in self.sems:
            self.sems[key] = self.es.enter_context(self.nc.semaphore(key))
            self.dcnt[key] = 0
        waits = self._waits(q, reads, writes, skip=key)
        self.dcnt[key] += 16
        tok = (key, self.dcnt[key])
        self.prog[q].append(("dma", waits, (out, in_, kw), key))
        self._mark(tok, reads, writes)
        return tok

    def barrier(self):
        toks = [(e, self.cnt[e]) for e in self.ENG if self.cnt[e] > 0]
        toks += [(kk, v) for kk, v in self.dcnt.items() if v > 0]
        for e in self.ENG:
            self.wait_all(e, [t for t in toks if t[0] != e])

    def wait_all(self, eng, toks):
        waits = []
        for k, v in toks:
            if self.known[eng].get(k, 0) < v:
                self.known[eng][k] = v
                waits.append((k, v))
        self.prog[eng].append(("wait", waits, None, None))

    def emit(self):
        nc = self.nc
        sems = self.sems
        prog = self.prog

        def replay(ename, e):
            for kind, waits, payload, extra in prog[ename]:
                for k, v in waits:
                    e.wait_ge(sems[k], v)
                if kind == "op":
                    ins = payload(e)
                    if extra:
                        ins.then_inc(sems[ename], 1)
                elif kind == "dma":
                    out, in_, kw = payload
                    e.dma_start(out=out, in_=in_, **kw).then_inc(sems[extra], 16)

        with nc.Block() as block:
            @block.tensor
            def _(e):
                replay("pe", e)

            @block.vector
            def _(e):
                replay("dve", e)

            @block.scalar
            def _(e):
                replay("act", e)

            @block.gpsimd
            def _(e):
                replay("pool", e)

            @block.sync
            def _(e):
                replay("sp", e)


class K:
    def __init__(self, nc, es):
        self.nc = nc
        self.es = es
        self.S = Sched(nc, es)
        self.outs = []

    def sb(self, name, shape, dt):
        t = self.es.enter_context(self.nc.sbuf_tensor("sb_" + name, shape, dt))
        return t, self.S.buf(name)

    def ps(self, name, shape, dt):
        return self.es.enter_context(self.nc.psum_tensor("ps_" + name, shape, dt))

    def mm(self, out, lhsT, rhs, start, stop, reads, writes, signal=True, skip=False):
        if skip:
            return self.S.op("pe", lambda e: e.matmul(out, lhsT=lhsT, rhs=rhs, start=False, stop=False,
                                                      skip_group_check=True), reads, writes, signal)
        return self.S.op("pe", lambda e: e.matmul(out, lhsT=lhsT, rhs=rhs, start=start, stop=stop),
                         reads, writes, signal)

    def tr(self, out, in_, ident, reads, writes, signal=True):
        return self.S.op("pe", lambda e: e.transpose(out, in_, ident), reads, writes, signal)

    def act(self, out, in_, func, reads, writes, bias=None, scale=1.0, accum_out=None):
        def fn(e):
            kw = {}
            if bias is not None:
                kw["bias"] = bias
            if accum_out is not None:
                kw["accum_out"] = accum_out
            return e.activation(out=out, in_=in_, func=func, scale=scale, **kw)
        return self.S.op("act", fn, reads, writes)

    def tt(self, eng, out, in0, in1, op, reads, writes):
        return self.S.op(eng, lambda e: e.tensor_tensor(out=out, in0=in0, in1=in1, op=op), reads, writes)

    def ts(self, eng, out, in0, s1, op0, reads, writes, s2=None, op1=None):
        if op1 is None:
            return self.S.op(eng, lambda e: e.tensor_scalar(out=out, in0=in0, scalar1=s1, scalar2=None, op0=op0),
                             reads, writes)
        return self.S.op(eng, lambda e: e.tensor_scalar(out=out, in0=in0, scalar1=s1, scalar2=s2, op0=op0, op1=op1),
                         reads, writes)

    def stt(self, out, in0, scalar, in1, op0, op1, reads, writes):
        return self.S.op("dve", lambda e: e.scalar_tensor_tensor(out=out, in0=in0, scalar=scalar, in1=in1,
                                                                 op0=op0, op1=op1), reads, writes)

    def cp(self, eng, out, in_, reads, writes):
        if eng == "act":
            return self.S.op("act", lambda e: e.copy(out=out, in_=in_), reads, writes)
        return self.S.op(eng, lambda e: e.tensor_copy(out=out, in_=in_), reads, writes)

    def memset(self, eng, ap, val, writes):
        return self.S.op(eng, lambda e: e.memset(ap, val), (), writes)

    def recip(self, out, in_, reads, writes):
        return self.S.op("dve", lambda e: e.reciprocal(out=out, in_=in_), reads, writes)

    def rstd(self, out, ss, inv_n, eps_t, reads, writes, extra_reads=()):
        self.act(out, ss, AF.Ln, list(reads) + list(extra_reads), writes, bias=eps_t, scale=inv_n)
        self.act(out, out, AF.Exp, writes, writes, scale=-0.5)


def build_phase_a(nc, es, k, x, w_in, cvec, ctab, cmat, ho, nblk=NB, stage=3, npass=1, own_from=0, vmask=None):
    S = k.S
    PB = 128

    ident, b_ident = k.sb("ident", [PB, PB], BF16)
    U, b_U = k.sb("U", [PB, PB], F32)
    SU, b_SU = k.sb("SU", [PB, PB], F32)
    Ub, b_Ub = k.sb("Ub", [PB, PB], BF16)
    ONES, b_ONES = k.sb("ONES", [PB, PB], F32)
    IDF, b_IDF = k.sb("IDF", [PB, PB], F32)
    gpm, b_gpm = k.sb("gpm", [PB, 16], F32)
    cw, b_cw = k.sb("cw", [PB, 6, 4], F32)
    dtb, b_dtb = k.sb("dtb", [PB, 8], F32)
    nA, b_nA = k.sb("nA", [PB, 8], F32)
    gnw, b_gnw = k.sb("gnw", [PB, PB], F32)
    sbw, b_sbw = k.sb("sbw", [PB, PB], F32)
    lamv, b_lamv = k.sb("lamv", [PB, 4, 64], F32)
    eps_t, b_eps = k.sb("eps_t", [PB, 1], F32)
    one_t, b_one = k.sb("one_t", [PB, 1], F32)
    nlam, b_nlam = k.sb("nlam", [PB, 1], F32)
    lt, b_lt = k.sb("lt", [PB, 4], F32)
    ljunk, b_ljunk = k.sb("ljunk", [PB, 64], F32)

    cbufs = [b_ident, b_U, b_SU, b_Ub, b_ONES, b_IDF, b_gpm, b_gnw, b_sbw, b_lamv]
    loads = [(ident[:], cmat["ident"], b_ident), (U[:], cmat["U"], b_U), (SU[:], cmat["SU"], b_SU),
             (Ub[:], cmat["Ub"], b_Ub), (ONES[:], cmat["ONES"], b_ONES), (IDF[:], cmat["IDF"], b_IDF),
             (gpm[:], cvec["gpm"], b_gpm),
             (gnw[:], cvec["gnw"].partition_broadcast(PB), b_gnw),
             (sbw[:], cvec["sbw"].partition_broadcast(PB), b_sbw)]
    for j, nm in enumerate(["lq1", "lk1", "lq2", "lk2"]):
        loads.append((lamv[:, j, :], cvec[nm].partition_broadcast(PB), b_lamv))
    tok = None
    for o, i_, bb in loads:
        tok = S.dma("sp", o, i_, writes=[bb], key="d_const")
    for bb in cbufs:
        bb.w = {tok[0]: tok[1]}
    k.memset("pool", eps_t[:], EPS, [b_eps])
    k.memset("pool", one_t[:], 1.0, [b_one])
    k.ts("dve", sbw[:], sbw[:], 1.0 - LAMBDA_INIT, ALU.mult, [b_sbw], [b_sbw])
    for j in range(2):
        k.tt("dve", ljunk[:], lamv[:, 2 * j, :], lamv[:, 2 * j + 1, :], ALU.mult, [b_lamv], [b_ljunk])
        S.op("dve", lambda e, j=j: e.reduce_sum(out=lt[:, j:j + 1], in_=ljunk[:], axis=AX.X), [b_ljunk], [b_lt])
    k.act(lt[:, 0:2], lt[:, 0:2], AF.Exp, [b_lt], [b_lt])
    k.tt("dve", lt[:, 2:3], lt[:, 1:2], lt[:, 0:1], ALU.subtract, [b_lt], [b_lt])
    k.ts("dve", nlam[:], lt[:, 2:3], -LAMBDA_INIT, ALU.add, [b_lt], [b_nlam])

    w_bufs = [k.sb(f"w_sb{j}", [PB, 16, WC], BF16) for j in range(1)]
    w_sb, b_w = w_bufs[0]
    wcur = [w_sb, b_w]

    x_t, b_xt = k.sb("x_t", [PB, D], F32)
    xs_b, b_xs = k.sb("xs_b", [PB, D], BF16)
    ss, b_ss = k.sb("ss", [PB, 2], F32)
    hT, b_hT = k.sb("hT", [PB, 16, 512], BF16)
    kT = []
    for m in range(2):
        kT.append(k.sb(f"kT{m}", [PB, SEQ], BF16))
    V, b_V = k.sb("V", [PB, 32, 2, 130], BF16)
    b_Vb = [S.buf(f"Vb{j}") for j in range(NB)]
    b_kTb = [[S.buf(f"kTb{m}_{j}") for j in range(NB)] for m in range(2)]
    k.memset("pool", V[:], 1.0, b_Vb)
    if vmask is not None:
        vm, b_vm = k.sb("vm", [PB, 32], F32)
        S.dma("sp", vm[:], vmask, writes=[b_vm])
        for m_ in range(2):
            k.cp("dve", V[:, :, m_, 128], vm[:], [b_vm], b_Vb)
    qTp = [[k.sb(f"qT{p_}_{m}", [PB, 512], BF16) for m in range(2)] for p_ in range(2)]
    cosb, b_cos = k.sb("cosb", [PB, 512], F32)
    sinb, b_sin = k.sb("sinb", [PB, 512], F32)
    st = [k.sb(f"st{t}", [PB, 515], F32) for t in range(6)]
    cacc, b_cacc = k.sb("cacc", [PB, 512], F32)
    cy, b_cy = k.sb("cy", [PB, 512], F32)
    r1, b_r1 = cacc, b_cacc
    r2, b_r2 = cy, b_cy
    csq, b_csq = k.sb("csq", [PB, 512], F32)
    crn, b_crn = k.sb("crn", [PB, 512], F32)
    gqp = [[k.sb(f"gq{p_}_{h}", [PB, 512], F32R) for h in range(2)] for p_ in range(2)]
    gkp = [[k.sb(f"gk{p_}_{h}", [PB, 512], F32R) for h in range(2)] for p_ in range(2)]
    gvp = [[k.sb(f"gv{p_}_{h}", [PB, 512], F32R) for h in range(2)] for p_ in range(2)]
    gt6p = [[gqp[p_][0], gqp[p_][1], gkp[p_][0], gkp[p_][1], gvp[p_][0], gvp[p_][1]] for p_ in range(2)]
    pT = [k.sb(f"pT{j}", [PB, 512], BF16) for j in range(2)]
    zsp = [k.sb(f"zs{p_}", [PB, 4, 256], BF16) for p_ in range(2)]
    ztmp, b_ztmp = csq, b_csq
    bap = [k.sb(f"ba{p_}", [PB, 4, 4], F32) for p_ in range(2)]
    outblk, b_ob = k.sb("outblk", [PB, 4, 512], BF16)
    o1s, b_o1s = k.sb("o1s", [PB, 4, PB], F32)
    dfin, b_dfin = k.sb("dfin", [PB, PB], F32)
    fjunk, b_fjunk = k.sb("fjunk", [PB, PB], F32)
    fs, b_fs = k.sb("fs", [PB, 4], F32)

    beta, b_beta = k.sb("beta", [PB, 8], F32)
    nbeta, b_nbeta = k.sb("nbeta", [PB, 8], F32)
    gt, b_gt = k.sb("gt", [PB, 8], F32)
    gc, b_gc = k.sb("gc", [PB, 8], F32)
    ngc, b_ngc = k.sb("ngc", [PB, 8], F32)
    gtot, b_gtot = k.sb("gtot", [PB, 8], F32)
    egc, b_egc = k.sb("egc", [PB, 8], F32)
    eke, b_eke = k.sb("eke", [PB, 8], F32)
    egl, b_egl = k.sb("egl", [PB, 8], F32)
    GT = []
    for h in range(2):
        T_ = {}
        for nm in ("Gtri", "dm", "decU", "decSU", "ub", "o1g", "og"):
            T_[nm] = k.sb(f"{nm}{h}", [PB, PB], F32)
        T_["fjg"] = T_["dm"]
        for nm in ("PTf", "qkT", "kg", "ke", "vt", "WT", "vnew", "Sr"):
            T_[nm] = k.sb(f"{nm}{h}", [PB, PB], F32R)
        T_["Xb"] = [k.sb(f"Xb{h}_{j}", [PB, PB], F32R) for j in range(2)]
        T_["Yb"] = [k.sb(f"Yb{h}_{j}", [PB, PB], F32R) for j in range(2)]
        T_["fsg"] = k.sb(f"fsg{h}", [PB, 4], F32)
        GT.append(T_)
    b_obg = [S.buf("obg0"), S.buf("obg1")]
    Sf = [k.sb(f"Sf{h}", [PB, PB], F32) for h in range(2)]

    ptr = k.ps("ptr", [PB, 1024], BF16)
    b_ptr0 = S.buf("ptr0")
    b_ptr = [b_ptr0, b_ptr0]
    pin = [k.ps(f"pin{j}", [PB, 512], F32) for j in range(2)]
    b_pin = [S.buf(f"pin{j}") for j in range(2)]
    pst = [k.ps(f"pst{j}", [PB, 512], F32) for j in range(2)]
    b_pst = [S.buf(f"pst{j}") for j in range(2)]
    pacc = [k.ps(f"pacc{j}", [PB, 512], F32) for j in range(2)]
    b_paccb = [S.buf(f"pacc{c}") for c in range(2)]
    b_pacc = [b_paccb[0], b_paccb[0], b_paccb[0], b_paccb[1]]
    pg = k.ps("pg", [PB, 512], F32)
    b_pg = S.buf("pg")
    pgi = [0, 0]

    def pgslot(hd, own=True):
        j = pgi[hd] % 2
        pgi[hd] += 1
        if hd == 0:
            return (pg[:, 0:PB], b_pg) if j == 0 else (pg[:, 128:2 * PB], b_pg)
        if own:
            return (pg[:, 256:256 + PB], b_pg) if j == 0 else (pg[:, 384:384 + PB], b_pg)
        return (pacc[1][:, 256:256 + PB], b_paccb[1]) if j == 0 else (pacc[1][:, 384:384 + PB], b_paccb[1])

    def acc_ap(c):
        if c < 3:
            return pacc[0][:, c * 130:c * 130 + 129]
        return pacc[1][:, 0:129]

    xv = x.rearrange("(t p) d -> t p d", p=PB)

    def load_weights(g):
        w_t, b_t = w_bufs[g % len(w_bufs)]
        wv = w_in[g].rearrange("(kc p) c -> p kc c", p=PB)
        tok = None
        for kc in range(16):
            tok = S.dma("pool", w_t[:, kc, :], wv[:, kc, :], writes=[b_t], key=f"d_w{g % len(w_bufs)}")
        b_t.w = {tok[0]: tok[1]}
        for kc in range(16):
            k.ts("pool" if kc % 2 else "dve", w_t[:, kc, :], w_t[:, kc, :], gpm[:, kc:kc + 1], ALU.mult,
                 [b_t, b_gpm], [b_t])

    loaded = [-1]

    def pass_setup(g):
        if loaded[0] != g:
            load_weights(g)
            loaded[0] = g
        wcur[0], wcur[1] = w_bufs[g % len(w_bufs)]
        S.dma("sp", cw[:], cvec["cw"][g], writes=[b_cw])
        S.dma("sp", dtb[:], cvec["dtb8"][g].partition_broadcast(PB), writes=[b_dtb])
        S.dma("sp", nA[:], cvec["alog8"][g].partition_broadcast(PB), writes=[b_nA])
        k.act(nA[:], nA[:], AF.Exp, [b_nA], [b_nA])
        k.ts("dve", nA[:], nA[:], -1.0, ALU.mult, [b_nA], [b_nA])
        for t in range(6):
            k.memset("pool", st[t][0][:, 0:3], 0.0, [st[t][1]])
        for h in range(2):
            k.memset("pool", Sf[h][0][:], 0.0, [Sf[h][1]])
            k.cp("dve", GT[h]["Sr"][0][:], Sf[h][0][:], [Sf[h][1]], [GT[h]["Sr"][1]])
    pin_i = [0]

    def next_pin():
        j = pin_i[0] % 2
        pin_i[0] += 1
        return pin[j], b_pin[j]

    pst_i = [0]
    pT_i = [0]

    def front_gen(g, blk):
        w_sb, b_w = wcur
        own = blk >= own_from
        par = blk % 2
        qT = qTp[par]
        gt6 = gt6p[par]
        zs, b_zs = zsp[par]
        ba, b_ba = bap[par]
        for tt in range(4):
            T = 4 * blk + tt
            S.dma("sp", x_t[:], xv[T], writes=[b_xt])
            k.act(xs_b[:], x_t[:], AF.Square, [b_xt], [b_xs, b_ss], accum_out=ss[:, 0:1])
            k.rstd(ss[:, 1:2], ss[:, 0:1], 1.0 / D, eps_t[:], [b_ss], [b_ss], extra_reads=[b_eps])
            k.ts("dve", xs_b[:], x_t[:], ss[:, 1:2], ALU.mult, [b_xt, b_ss], [b_xs])
            for half in range(2):
                for j in range(8):
                    kc = half * 8 + j
                    k.tr(ptr[:, j * PB:(j + 1) * PB],
                         xs_b[:, kc * PB:(kc + 1) * PB], ident[:], [b_xs, b_ident], [b_ptr0],
                         signal=(j == 7))
                k.cp("act" if half else "dve", hT[:, half * 8:(half + 1) * 8, tt * PB:(tt + 1) * PB],
                     ptr[:, :].rearrange("p (j c) -> p j c", c=PB), [b_ptr0], [b_hT])
                yield
        S.dma("sp", cosb[:], ctab["cos"][:, blk * 512:(blk + 1) * 512], writes=[b_cos])
        S.dma("sp", sinb[:], ctab["sin"][:, blk * 512:(blk + 1) * 512], writes=[b_sin])

        for gi in range(10 if stage >= 0.6 else (4 if stage >= 0.4 else 0)):
            if not own and (gi in (0, 1) or (gi in (4, 5) and blk != own_from - 1)):
                continue
            pb_, bpb = next_pin()
            for kc in range(16):
                k.mm(pb_[:], w_sb[:, kc, gi * PB:(gi + 1) * PB], hT[:, kc, :], kc == 0, kc == 15,
                     [b_w, b_hT], [bpb], signal=(kc == 15))
            yield
            if gi < 4:
                k.tt("dve", r1[:], pb_[:], cosb[:], ALU.mult, [bpb, b_cos], [b_r1])
                for (o0, i0) in ((0, 32), (32, 0), (64, 96), (96, 64)):
                    k.tt("dve", r2[o0:o0 + 32, :], pb_[i0:i0 + 32, :], sinb[i0:i0 + 32, :], ALU.mult,
                         [bpb, b_sin], [b_r2])
                if gi < 2:
                    dst, bdst = qT[gi][0][:], qT[gi][1]
                else:
                    dst, bdst = kT[gi - 2][0][:, blk * 512:(blk + 1) * 512], b_kTb[gi - 2][blk]
                k.tt("pool", dst, r1[:], r2[:], ALU.add, [b_r1, b_r2], [bdst])
            else:
                tix = gi - 4
                stt_, bst = st[tix]
                k.cp("act", stt_[:, 3:515], pb_[:], [bpb], [bst])
                k.ts("dve", cacc[:], stt_[:, 3:515], cw[:, tix, 3:4], ALU.mult, [bst, b_cw], [b_cacc])
                for j in (2, 1, 0):
                    k.stt(cacc[:], stt_[:, j:j + 512], cw[:, tix, j:j + 1], cacc[:], ALU.mult, ALU.add,
                          [bst, b_cw, b_cacc], [b_cacc])
                k.cp("pool", stt_[:, 0:3], stt_[:, 512:515], [bst], [bst])
                dstt, bdst = gt6[tix]
                if tix >= 4:
                    k.act(dstt[:], cacc[:], AF.Silu, [b_cacc], [bdst])
                else:
                    k.act(cy[:], cacc[:], AF.Silu, [b_cacc], [b_cy])
                    k.tt("pool", csq[:], cy[:], cy[:], ALU.mult, [b_cy], [b_csq])
                    pn, bpn = next_pin()
                    k.mm(pn[:], ONES[:], csq[:], True, True, [b_ONES, b_csq], [bpn])
                    k.act(crn[:], pn[:], AF.Ln, [bpn, b_eps], [b_crn], bias=eps_t[:], scale=1.0)
                    k.act(crn[:], crn[:], AF.Exp, [b_crn], [b_crn], scale=-0.5)
                    if tix < 2:
                        k.stt(dstt[:], cy[:], 128.0 ** -0.5, crn[:], ALU.mult, ALU.mult, [b_cy, b_crn], [bdst])
                    else:
                        k.tt("dve", dstt[:], cy[:], crn[:], ALU.mult, [b_cy, b_crn], [bdst])
        for tt in range(4 if stage >= 0.8 else 0):
            T = 4 * blk + tt
            pb_, bpb = next_pin()
            for kc in range(16):
                k.mm(pb_[:, 0:260], hT[:, kc, tt * PB:(tt + 1) * PB], w_sb[:, kc, 1280:1540], kc == 0, kc == 15,
                     [b_w, b_hT], [bpb], signal=(kc == 15))
            import os
            if os.environ.get('DBG_V', '1') == '1':
                k.cp("dve", V[:, T, :, 0:128], pb_[:, 0:256].rearrange("p (h e) -> p h e", e=128), [bpb], [b_Vb[blk]])
            k.cp("dve", ba[:, tt, :], pb_[:, 256:260], [bpb], [b_ba])
            if own:
                pb2, bpb2 = next_pin()
                for kc in range(16):
                    k.mm(pb2[:, 0:256], hT[:, kc, tt * PB:(tt + 1) * PB], w_sb[:, kc, 1540:1796], kc == 0, kc == 15,
                         [b_w, b_hT], [bpb2], signal=(kc == 15))
                k.cp("dve", ztmp[:, 0:256], pb2[:, 0:256], [bpb2], [b_ztmp])
                k.act(zs[:, tt, :], ztmp[:, 0:256], AF.Silu, [b_ztmp], [b_zs])
            yield

    def back_gen(g, blk):
        own = blk >= own_from
        par = blk % 2
        hov = ho[g].rearrange("(t p) c -> p t c", p=PB)
        qT = qTp[par]
        gq, gk, gv = gqp[par], gkp[par], gvp[par]
        zs, b_zs = zsp[par]
        ba, b_ba = bap[par]
        if stage < 1:
            k.outs.append(S.dma("sp", hov[:, 4 * blk:4 * blk + 4, :], outblk[:], reads=[b_ob], key="d_out"))
            return
        bav = ba[:, :, 0:2]
        aav = ba[:, :, 2:4]
        k.act(beta[:].rearrange("p (t h) -> p t h", h=2), bav, AF.Sigmoid, [b_ba], [b_beta])
        k.ts("dve", nbeta[:], beta[:], -1.0, ALU.mult, [b_beta], [b_nbeta])
        k.tt("dve", gt[:].rearrange("p (t h) -> p t h", h=2), aav, dtb[:].rearrange("p (t h) -> p t h", h=2),
             ALU.add, [b_ba, b_dtb], [b_gt])
        k.act(gt[:], gt[:], AF.Exp, [b_gt], [b_gt])
        k.act(gt[:], gt[:], AF.Ln, [b_gt, b_one], [b_gt], bias=one_t[:], scale=1.0)
        k.tt("dve", gt[:], gt[:], nA[:], ALU.mult, [b_gt, b_nA], [b_gt])
        ps1, bps1 = pgslot(0)
        k.mm(ps1[:, 0:8], U[:], gt[:], True, True, [b_U, b_gt], [bps1])
        k.cp("dve", gc[:], ps1[:, 0:8], [bps1], [b_gc])
        ps2, bps2 = pgslot(0)
        k.mm(ps2[:, 0:8], ONES[:], gt[:], True, True, [b_ONES, b_gt], [bps2])
        k.cp("dve", gtot[:], ps2[:, 0:8], [bps2], [b_gtot])
        k.ts("dve", ngc[:], gc[:], -1.0, ALU.mult, [b_gc], [b_ngc])
        k.act(egc[:], gc[:], AF.Exp, [b_gc], [b_egc])
        k.act(egl[:], gtot[:], AF.Exp, [b_gtot], [b_egl])
        k.tt("dve", eke[:], gtot[:], gc[:], ALU.subtract, [b_gtot, b_gc], [b_eke])
        k.act(eke[:], eke[:], AF.Exp, [b_eke], [b_eke])

        def attn_gen(blk=blk):
            nkt = 4 * (blk + 1)
            for h in range(4 if (stage >= 2 and own) else 0):
                m, comp = h // 2, h % 2
                r0 = comp * 64
                S.op("dve", lambda e: e.memset(pacc[0][:, 0:390], 0.0), (), [b_paccb[0]])
                S.op("dve", lambda e: e.memset(pacc[1][:, 0:130], 0.0), (), [b_paccb[1]])
                for kt in range(nkt):
                    j = kt - 4 * blk
                    c0 = max(j, 0) * PB
                    sj = pst_i[0] % 2
                    pst_i[0] += 1
                    k.mm(pst[sj][:, c0:512], kT[m][0][r0:r0 + 64, kt * PB:(kt + 1) * PB],
                         qT[m][0][r0:r0 + 64, c0:512], True, True, [b_kTb[m][kt // 4], qT[m][1]], [b_pst[sj]])
                    pj = pT_i[0] % 2
                    pT_i[0] += 1
                    pTt, bpT = pT[pj]
                    k.act(pTt[:, c0:512], pst[sj][:, c0:512], AF.Exp, [b_pst[sj]], [bpT], scale=0.125)
                    if j >= 0:
                        k.tt("dve", pTt[:, c0:c0 + PB], pTt[:, c0:c0 + PB], Ub[:], ALU.mult, [bpT, b_Ub], [bpT])
                    cs = list(range(max(j, 0), 4))
                    for c in cs:
                        k.mm(acc_ap(c), pTt[:, c * PB:(c + 1) * PB], V[:, kt, m, 0:129], False, False,
                             [bpT, b_Vb[kt // 4]], [b_pacc[c]], signal=(c == cs[-1]), skip=True)
                    yield
                for c in range(4):
                    a = acc_ap(c)
                    k.recip(fs[:, 0:1], a[:, 128:129], [b_pacc[c]], [b_fs])
                    if comp == 0:
                        k.ts("dve", o1s[:, c, :], a[:, 0:128], fs[:, 0:1], ALU.mult, [b_pacc[c], b_fs], [b_o1s])
                    else:
                        k.tt("dve", fs[:, 1:2], fs[:, 0:1], nlam[:], ALU.mult, [b_fs, b_nlam], [b_fs])
                        k.stt(dfin[:], a[:, 0:128], fs[:, 1:2], o1s[:, c, :], ALU.mult, ALU.add,
                              [b_pacc[c], b_fs, b_o1s], [b_dfin])
                        k.act(fjunk[:], dfin[:], AF.Square, [b_dfin], [b_fjunk, b_fs], accum_out=fs[:, 2:3])
                        k.rstd(fs[:, 3:4], fs[:, 2:3], 1.0 / 128, eps_t[:], [b_fs], [b_fs], extra_reads=[b_eps])
                        k.stt(outblk[:, c, m * PB:(m + 1) * PB], dfin[:], fs[:, 3:4], sbw[:], ALU.mult, ALU.mult,
                              [b_dfin, b_fs, b_sbw], [b_ob])
                    yield

        def gdn_gen(hd, blk=blk):
            T_ = GT[hd]
            Gtri, b_Gtri = T_["Gtri"]; dm, b_dm = T_["dm"]; decU, b_decU = T_["decU"]; decSU, b_decSU = T_["decSU"]
            Xb = T_["Xb"]; Yb = T_["Yb"]; PTf, b_PTf = T_["PTf"]; qkT, b_qkT = T_["qkT"]; kg, b_kg = T_["kg"]
            ke, b_ke = T_["ke"]; vt, b_vt = T_["vt"]; ub, b_ub = T_["ub"]; WT, b_WT = T_["WT"]
            vnew, b_vnew = T_["vnew"]; o1g, b_o1g = T_["o1g"]; og, b_og = T_["og"]; Sr, bSr = T_["Sr"]
            fsg, b_fsg = T_["fsg"]; fjg, b_fjg = T_["fjg"]
            gqt, bgq = gq[hd]
            gkt, bgk = gk[hd]
            gvt, bgv = gv[hd]
            Sft, bSf = Sf[hd]
            for cc in range(4 if stage >= 3 else 0):
                csl = slice(cc * PB, (cc + 1) * PB)
                q8 = cc * 2 + hd
                sc = lambda t, q8=q8: t[:, q8:q8 + 1]
                k.ts("dve", Gtri[:], U[:], sc(gt), ALU.mult, [b_U, b_gt], [b_Gtri])
                p_a, bp_a = pgslot(hd, own)
                k.mm(p_a, ONES[:], Gtri[:], True, True, [b_ONES, b_Gtri], [bp_a])
                k.ts("dve", dm[:], p_a, sc(ngc), ALU.add, [bp_a, b_ngc], [b_dm], s2=0.0, op1=ALU.min)
                k.act(dm[:], dm[:], AF.Exp, [b_dm], [b_dm])
                k.tt("dve", decU[:], dm[:], U[:], ALU.mult, [b_dm, b_U], [b_decU])
                k.tt("dve", decSU[:], dm[:], SU[:], ALU.mult, [b_dm, b_SU], [b_decSU])
                yield
                p_b, bp_b = pgslot(hd, own)
                k.mm(p_b, gkt[:, csl], gkt[:, csl], True, True, [bgk], [bp_b])
                k.stt(PTf[:], p_b, sc(nbeta), decSU[:], ALU.mult, ALU.mult, [bp_b, b_nbeta, b_decSU], [b_PTf])
                k.cp("dve", Yb[0][0][:], PTf[:], [b_PTf], [Yb[0][1]])
                k.tt("dve", PTf[:], PTf[:], IDF[:], ALU.add, [b_PTf, b_IDF], [b_PTf])
                yield
                if own:
                    p_c, bp_c = pgslot(hd, own)
                    k.mm(p_c, gkt[:, csl], gqt[:, csl], True, True, [bgk, bgq], [bp_c])
                    k.tt("dve", qkT[:], p_c, decU[:], ALU.mult, [bp_c, b_decU], [b_qkT])
                    yield
                p_t, bp_t = pgslot(hd, own)
                k.tr(p_t, Yb[0][0][:].bitcast(F32), IDF[:], [Yb[0][1], b_IDF], [bp_t])
                k.cp("act", Xb[0][0][:], p_t, [bp_t], [Xb[0][1]])
                yield
                p_k, bp_k = pgslot(hd, own)
                k.tr(p_k, gkt[:, csl].bitcast(F32), IDF[:], [bgk, b_IDF], [bp_k])
                k.ts("dve", kg[:], p_k, sc(egc), ALU.mult, [bp_k, b_egc], [b_kg])
                k.ts("dve", ke[:], p_k, sc(eke), ALU.mult, [bp_k, b_eke], [b_ke])
                yield
                p_v, bp_v = pgslot(hd, own)
                k.tr(p_v, gvt[:, csl].bitcast(F32), IDF[:], [bgv, b_IDF], [bp_v])
                k.cp("act", vt[:], p_v, [bp_v], [b_vt])
                yield
                cur = 0
                for lvl in range(6):
                    nxt = 1 - cur
                    Xc, bXc = Xb[cur]
                    Yc, bYc = Yb[cur]
                    Xn, bXn = Xb[nxt]
                    Yn, bYn = Yb[nxt]
                    p_x, bp_x = pgslot(hd, own)
                    k.mm(p_x, Yc[:], Xc[:], True, True, [bYc, bXc], [bp_x])
                    k.cp("act", Xn[:], p_x, [bp_x], [bXn])
                    yield
                    if lvl < 5:
                        p_y, bp_y = pgslot(hd, own)
                        k.mm(p_y, Xc[:], Yc[:], True, True, [bYc, bXc], [bp_y])
                        k.cp("dve", Yn[:], p_y, [bp_y], [bYn])
                        yield
                    p_p, bp_p = pgslot(hd, own)
                    k.mm(p_p, Xn[:], PTf[:], True, True, [bXn, b_PTf], [bp_p])
                    k.tt("dve", PTf[:], p_p, PTf[:], ALU.add, [bp_p, b_PTf], [b_PTf])
                    yield
                    cur = nxt
                p_u, bp_u = pgslot(hd, own)
                k.mm(p_u, PTf[:], vt[:], True, True, [b_PTf, b_vt], [bp_u])
                k.ts("dve", ub[:], p_u, sc(beta), ALU.mult, [bp_u, b_beta], [b_ub])
                yield
                p_w, bp_w = pgslot(hd, own)
                k.mm(p_w, kg[:], PTf[:], True, True, [b_kg, b_PTf], [bp_w])
                k.cp("act", WT[:], p_w, [bp_w], [b_WT])
                yield
                p_i, bp_i = pgslot(hd, own)
                k.mm(p_i, WT[:], Sr[:], True, True, [b_WT, bSr], [bp_i])
                k.stt(vnew[:], p_i, sc(nbeta), ub[:], ALU.mult, ALU.add, [bp_i, b_nbeta, b_ub], [b_vnew])
                yield
                if own:
                    p_o1, bp_o1 = pgslot(hd, own)
                    k.mm(p_o1, gqt[:, csl], Sr[:], True, True, [bgq, bSr], [bp_o1])
                    k.ts("dve", o1g[:], p_o1, sc(egc), ALU.mult, [bp_o1, b_egc], [b_o1g])
                    yield
                    p_o2, bp_o2 = pgslot(hd, own)
                    k.mm(p_o2, qkT[:], vnew[:], True, True, [b_qkT, b_vnew], [bp_o2])
                    k.tt("dve", og[:], p_o2, o1g[:], ALU.add, [bp_o2, b_o1g], [b_og])
                    yield
                p_s, bp_s = pgslot(hd, own)
                k.mm(p_s, ke[:], vnew[:], True, True, [b_ke, b_vnew], [bp_s])
                k.stt(Sr[:], Sft[:], sc(egl), p_s, ALU.mult, ALU.add, [bSf, b_egl, bp_s], [bSr])
                k.stt(Sft[:], Sft[:], sc(egl), p_s, ALU.mult, ALU.add, [bSf, b_egl, bp_s], [bSf])
                yield
                if not own:
                    continue
                k.act(fjg[:], og[:], AF.Square, [b_og], [b_fjg, b_fsg], accum_out=fsg[:, 2:3])
                k.rstd(fsg[:, 3:4], fsg[:, 2:3], 1.0 / 128, eps_t[:], [b_fsg], [b_fsg], extra_reads=[b_eps])
                k.stt(og[:], og[:], fsg[:, 3:4], gnw[:], ALU.mult, ALU.mult, [b_og, b_fsg, b_gnw], [b_og])
                k.tt("dve", outblk[:, cc, 256 + hd * PB:256 + (hd + 1) * PB], og[:],
                     zs[:, cc, hd * PB:(hd + 1) * PB], ALU.mult, [b_og, b_zs], [b_obg[hd]])
                yield

        gens = [attn_gen(), gdn_gen(0), gdn_gen(1)]
        while gens:
            for gn in list(gens):
                try:
                    next(gn)
                except StopIteration:
                    gens.remove(gn)
            yield
        if own:
            ob0 = 4 * (blk - own_from)
            k.outs.append(S.dma("sp", hov[:, ob0:ob0 + 4, :], outblk[:], reads=[b_ob, b_obg[0], b_obg[1]], key="d_out"))
    def run_all(gn):
        for _ in gn:
            pass

    RATIO = 6
    for g in range(npass):
        pass_setup(g)
        run_all(front_gen(g, 0))
        for blk in range(nblk):
            fr = front_gen(g, blk + 1) if blk + 1 < nblk else None
            bk = back_gen(g, blk)
            cnt_ = 0
            while bk is not None or fr is not None:
                if bk is not None:
                    try:
                        next(bk)
                    except StopIteration:
                        bk = None
                cnt_ += 1
                if fr is not None and (bk is None or cnt_ % RATIO == 0):
                    try:
                        next(fr)
                    except StopIteration:
                        fr = None
                        if blk + 1 == nblk - 1 and g + 1 < npass and own_from > 0:
                            load_weights(g + 1)
                            loaded[0] = g + 1
    if nblk == 0:
        pass_setup(0)
        hov = ho[0].rearrange("(t p) c -> p t c", p=PB)
        k.cp("dve", outblk[:, 0, :], w_sb[:, 0, 0:512], [b_w], [b_ob])
        k.cp("dve", outblk[:, 1, :], w_sb[:, 15, 0:512], [b_w], [b_ob])
        k.outs.append(S.dma("sp", hov[:, 0:4, :], outblk[:], reads=[b_ob], key="d_out"))


def _consts():
    bf = ml_dtypes.bfloat16
    r = np.arange(128)
    U = (r[None, :] >= r[:, None]).astype(np.float32)
    SU = (r[None, :] > r[:, None]).astype(np.float32)
    inv_freq = (10000.0 ** (-np.arange(0, 64, 2, dtype=np.float32) / np.float32(64))).astype(np.float32)
    ang = np.arange(SEQ, dtype=np.float32)[None, :] * inv_freq[:, None]
    fi = (r % 64) % 32
    cos = np.cos(ang.astype(np.float64)).astype(np.float32)[fi]
    sin = np.sin(ang.astype(np.float64)).astype(np.float32)[fi]
    sgn = np.where((r % 64) < 32, 1.0, -1.0).astype(np.float32)[:, None]
    return {"ident": np.eye(128).astype(bf), "U": U, "SU": SU, "Ub": U.astype(bf),
            "ONES": np.ones((128, 128), np.float32), "IDF": np.eye(128, dtype=np.float32),
            "cos": cos, "sin": (sin * sgn).astype(np.float32)}


def _phase_a_program(nblk=NB, stage=3):
    nc = bass.Bass("TRN2", target_bir_lowering=False)
    def din(name, shape, dt=F32):
        return nc.dram_tensor(name, shape, dt, kind="ExternalInput").ap()
    x = din("x", [SEQ, D])
    w_in = din("w_in", [1, D, WC])
    cvec = {"gpm": din("gpm", [128, 16]), "cw": din("cw", [1, 128, 6, 4]), "dtb8": din("dtb8", [1, 8]),
            "alog8": din("alog8", [1, 8]), "gnw": din("gnw", [128]), "sbw": din("sbw", [128]),
            "lq1": din("lq1", [64]), "lk1": din("lk1", [64]), "lq2": din("lq2", [64]), "lk2": din("lk2", [64])}
    ctab = {"cos": din("cos", [128, SEQ]), "sin": din("sin", [128, SEQ])}
    cmat = {"ident": din("ident", [128, 128], BF16), "U": din("U", [128, 128]), "SU": din("SU", [128, 128]),
            "Ub": din("Ub", [128, 128], BF16), "ONES": din("ONES", [128, 128]), "IDF": din("IDF", [128, 128])}
    ho = nc.dram_tensor("ho", [1, SEQ, 512], BF16, kind="ExternalOutput").ap()
    with contextlib.ExitStack() as es:
        k = K(nc, es)
        build_phase_a(nc, es, k, x, w_in, cvec, ctab, cmat, ho, nblk, stage)
        k.S.wait_all("sp", k.outs)
        k.S.emit()
    return nc


def _phase_a_inputs(inp):
    C = _consts()
    w = inp["w_in"][0]
    maps = []
    for r in range(8):
        b, g = r // 4, r % 4
        cols = []
        c256 = np.arange(256 * g, 256 * g + 256)
        cols += list(c256)
        cols += list(1024 + c256)
        cols += list(3072 + c256)
        cols += list(4096 + c256)
        cols += list(5120 + c256)
        cols += list(2048 + c256)
        cols += [7168 + 2 * g, 7169 + 2 * g, 7176 + 2 * g, 7177 + 2 * g]
        cols += list(6144 + c256)
        cols = np.asarray(cols)
        cwc = inp["conv_w"][0]
        ch = np.concatenate([c256, 1024 + c256, 2048 + c256])
        cw = np.ascontiguousarray(cwc[:, ch].reshape(4, 6, 128).transpose(2, 1, 0))
        hs = [2 * g, 2 * g + 1]
        m = {"x": np.ascontiguousarray(inp["x"][b]),
             "w_in": np.ascontiguousarray(w[:, cols])[None],
             "gpm": np.ascontiguousarray(inp["norm_pre_mix"][0].reshape(16, 128).T),
             "cw": cw[None],
             "dtb8": np.ascontiguousarray(np.tile(inp["dt_bias"][0][hs], 4))[None],
             "alog8": np.ascontiguousarray(np.tile(inp["a_log"][0][hs], 4))[None],
             "gnw": np.ascontiguousarray(inp["gdn_norm_w"][0]),
             "sbw": np.ascontiguousarray(inp["da_subln_w"][0]),
             "lq1": np.ascontiguousarray(inp["lambda_q1"][0]), "lk1": np.ascontiguousarray(inp["lambda_k1"][0]),
             "lq2": np.ascontiguousarray(inp["lambda_q2"][0]), "lk2": np.ascontiguousarray(inp["lambda_k2"][0])}
        m.update(C)
        maps.append(m)
    return maps


def run_phase_a(inp, nblk=NB, stage=3, ncores=8):
    nc = _phase_a_program(nblk, stage)
    maps = _phase_a_inputs(inp)
    import os
    tr = os.environ.get("KTRACE", "0") == "1"
    res = run_bass_kernel_spmd(nc, maps[:ncores], core_ids=list(range(ncores)), trace=tr)
    if tr:
        print("PHASE_A exec_time_ns", res.exec_time_ns)
    return [r["ho"][0] for r in res.results]


NT = 8


def build_phase_b(nc, es, k, mixin, xs, w_out, w_up, w_down, gpost, gpm2, gpost2, identd, out, ho4=None, qreg=None):
    S = k.S
    PB = 128
    ident, b_ident = k.sb("identB", [PB, PB], BF16)
    eps_t, b_eps = k.sb("epsB", [PB, 1], F32)
    g2, b_g2 = k.sb("g2", [PB, 16], F32)
    S.dma("sp", ident[:], identd, writes=[b_ident])
    S.dma("sp", g2[:], gpm2, writes=[b_g2])
    k.memset("pool", eps_t[:], EPS, [b_eps])
    Y, b_Yw = k.sb("Y", [PB, NT, D], F32)
    b_Y = [S.buf(f"Y{t}") for t in range(NT)]
    mT, b_mT = k.sb("mT", [PB, 16, NT * PB], BF16)
    wo = [k.sb(f"wo{j}", [PB, 16, 256], BF16) for j in range(2)]
    wu = [k.sb(f"wu{j}", [PB, 16, PB], BF16) for j in range(2)]
    wd = [k.sb(f"wd{j}", [PB, 8, 512], BF16) for j in range(2)]
    hidT, b_hid = k.sb("hidT", [PB, 8, NT * PB], BF16)
    x_t, b_xt = k.sb("xB", [PB, D], F32)
    xs_b, b_xs = k.sb("xsB", [PB, D], BF16)
    gB, b_gB = k.sb("gB", [PB, D], F32)
    rtmp, b_rtmp = k.sb("rtmp", [PB, 512], F32)
    ss, b_ss = k.sb("ssB", [PB, 2], F32)
    ptr = k.ps("ptrB", [PB, 1024], BF16)
    b_ptr = S.buf("ptrB")
    pin = [k.ps(f"pinB{j}", [PB, 512], F32) for j in range(2)]
    b_pin = [S.buf(f"pinB{j}") for j in range(2)]
    pst = [k.ps(f"pstB{j}", [PB, 512], F32) for j in range(2)]
    b_pst = [S.buf(f"pstB{j}") for j in range(2)]
    b_od = [S.buf(f"od{t}") for t in range(NT)]
    mv = mixin.rearrange("(t p) c -> t p c", p=PB) if mixin is not None else None
    xv = xs.rearrange("(t p) c -> t p c", p=PB)
    ov = out.rearrange("(t p) c -> t p c", p=PB)
    pin_i = [0]
    pst_i = [0]

    S.dma("sp", gB[:], gpost.partition_broadcast(PB), writes=[b_gB])
    for tt in range(NT):
        if ho4 is None:
            S.dma("sp", xs_b[:], mv[tt], writes=[b_xs])
        else:
            hv = ho4.rearrange("g (t p) c -> t p g c", p=PB)
            S.dma("sp", xs_b[:].rearrange("p (g c) -> p g c", g=4), hv[tt], writes=[b_xs])
        for half in range(2):
            for j in range(8):
                kc = half * 8 + j
                k.tr(ptr[:, j * PB:(j + 1) * PB], xs_b[:, kc * PB:(kc + 1) * PB], ident[:], [b_xs, b_ident], [b_ptr],
                     signal=(j == 7))
            k.cp("act" if half else "dve", mT[:, half * 8:(half + 1) * 8, tt * PB:(tt + 1) * PB],
                 ptr[:, :].rearrange("p (j c) -> p j c", c=PB), [b_ptr], [b_mT])
    wov = w_out.rearrange("(kc p) c -> p kc c", p=PB)
    for cg in range(8):
        wt, bw = wo[cg % 2]
        S.dma("pool", wt[:], wov[:, :, cg * 256:(cg + 1) * 256], writes=[bw])
        for tt in range(NT):
            j = pin_i[0] % 2
            pin_i[0] += 1
            for kc in range(16):
                k.mm(pin[j][:, 0:256], mT[:, kc, tt * PB:(tt + 1) * PB], wt[:, kc, :], kc == 0, kc == 15,
                     [b_mT, bw], [b_pin[j]], signal=(kc == 15))
            k.cp("act", Y[:, tt, cg * 256:(cg + 1) * 256], pin[j][:, 0:256], [b_pin[j]], [b_Y[tt]])
    for tt in range(NT):
        k.act(xs_b[:], Y[:, tt, :], AF.Square, [b_Y[tt]], [b_xs, b_ss], accum_out=ss[:, 0:1])
        k.rstd(ss[:, 1:2], ss[:, 0:1], 1.0 / D, eps_t[:], [b_ss], [b_ss], extra_reads=[b_eps])
        S.dma("sp", x_t[:], xv[tt], writes=[b_xt])
        k.stt(Y[:, tt, :], Y[:, tt, :], ss[:, 1:2], gB[:], ALU.mult, ALU.mult, [b_Y[tt], b_ss, b_gB], [b_Y[tt]])
        k.tt("pool", x_t[:], Y[:, tt, :], x_t[:], ALU.add, [b_Y[tt], b_xt], [b_xt])
        S.dma("sp", ov[tt], x_t[:], reads=[b_xt], writes=[b_od[tt]], key="d_x1st")
        k.act(xs_b[:], x_t[:], AF.Square, [b_xt], [b_xs, b_ss], accum_out=ss[:, 0:1])
        k.rstd(ss[:, 1:2], ss[:, 0:1], 1.0 / D, eps_t[:], [b_ss], [b_ss], extra_reads=[b_eps])
        k.ts("dve", xs_b[:], x_t[:], ss[:, 1:2], ALU.mult, [b_xt, b_ss], [b_xs])
        for half in range(2):
            for j in range(8):
                kc = half * 8 + j
                k.tr(ptr[:, j * PB:(j + 1) * PB], xs_b[:, kc * PB:(kc + 1) * PB], ident[:], [b_xs, b_ident], [b_ptr],
                     signal=(j == 7))
            for j in range(8):
                kc = half * 8 + j
                k.ts("dve", mT[:, kc, tt * PB:(tt + 1) * PB], ptr[:, j * PB:(j + 1) * PB], g2[:, kc:kc + 1], ALU.mult,
                     [b_ptr, b_g2], [b_mT])
    S.dma("sp", gB[:], gpost2.partition_broadcast(PB), writes=[b_gB])
    wuv = w_up.rearrange("(kc p) f -> p kc f", p=PB)
    for fg in range(8):
        for fc in range(8):
            F = fg * 8 + fc
            wt, bw = wu[F % 2]
            S.dma("pool", wt[:], wuv[:, :, F * PB:(F + 1) * PB], writes=[bw])
            for tb in range(2):
                j = pst_i[0] % 2
                pst_i[0] += 1
                for kc in range(16):
                    k.mm(pst[j][:], wt[:, kc, :], mT[:, kc, tb * 512:(tb + 1) * 512], kc == 0, kc == 15,
                         [bw, b_mT], [b_pst[j]], signal=(kc == 15))
                k.ts("dve", rtmp[:], pst[j][:], 0.0, ALU.max, [b_pst[j]], [b_rtmp])
                k.tt("pool", hidT[:, fc, tb * 512:(tb + 1) * 512], rtmp[:], rtmp[:], ALU.mult, [b_rtmp], [b_hid])
        wdv = w_down[fg * 1024:(fg + 1) * 1024, :].rearrange("(fc p) c -> p fc c", p=PB)
        for cg in range(4):
            wt, bw = wd[cg % 2]
            S.dma("pool", wt[:], wdv[:, :, cg * 512:(cg + 1) * 512], writes=[bw])
            for tt in range(NT):
                j = pin_i[0] % 2
                pin_i[0] += 1
                for fc in range(8):
                    k.mm(pin[j][:], hidT[:, fc, tt * PB:(tt + 1) * PB], wt[:, fc, :], fc == 0, fc == 7,
                         [b_hid, bw], [b_pin[j]], signal=(fc == 7))
                ysl = Y[:, tt, cg * 512:(cg + 1) * 512]
                if fg == 0:
                    k.cp("act", ysl, pin[j][:], [b_pin[j]], [b_Y[tt]])
                else:
                    k.tt("dve", ysl, pin[j][:], ysl, ALU.add, [b_pin[j], b_Y[tt]], [b_Y[tt]])
    for tt in range(NT):
        k.act(xs_b[:], Y[:, tt, :], AF.Square, [b_Y[tt]], [b_xs, b_ss], accum_out=ss[:, 0:1])
        k.rstd(ss[:, 1:2], ss[:, 0:1], 1.0 / D, eps_t[:], [b_ss], [b_ss], extra_reads=[b_eps])
        S.dma("sp", x_t[:], ov[tt], reads=[b_od[tt]], writes=[b_xt])
        k.stt(Y[:, tt, :], Y[:, tt, :], ss[:, 1:2], gB[:], ALU.mult, ALU.mult, [b_Y[tt], b_ss, b_gB], [b_Y[tt]])
        k.tt("pool", x_t[:], Y[:, tt, :], x_t[:], ALU.add, [b_Y[tt], b_xt], [b_xt])
        k.outs.append(S.dma("sp", ov[tt], x_t[:], reads=[b_xt], writes=[b_od[tt]], key="d_fin"))


def _phase_b_program():
    nc = bass.Bass("TRN2", target_bir_lowering=False)
    def din(name, shape, dt=F32):
        return nc.dram_tensor(name, shape, dt, kind="ExternalInput").ap()
    mixin = din("mixin", [NT * 128, D], BF16)
    xs = din("xs", [NT * 128, D])
    w_out = din("w_out", [D, D])
    w_up = din("w_up", [D, 4 * D])
    w_down = din("w_down", [4 * D, D])
    gpost = din("gpost", [D])
    gpm2 = din("gpm2", [128, 16])
    gpost2 = din("gpost2", [D])
    identd = din("ident", [128, 128], BF16)
    out = nc.dram_tensor("out", [NT * 128, D], F32, kind="ExternalOutput").ap()
    with contextlib.ExitStack() as es:
        k = K(nc, es)
        build_phase_b(nc, es, k, mixin, xs, w_out, w_up, w_down, gpost, gpm2, gpost2, identd, out)
        k.S.wait_all("sp", k.outs)
        k.S.emit()
    return nc


def _phase_b_inputs(inp, hos):
    bf = ml_dtypes.bfloat16
    maps = []
    ident = np.eye(128).astype(bf)
    for r in range(8):
        b, q = r // 4, r % 4
        rows = slice(q * 1024, (q + 1) * 1024)
        mix = np.concatenate([hos[4 * b + g][rows, 0:256] for g in range(4)] +
                             [hos[4 * b + g][rows, 256:512] for g in range(4)], axis=1)
        maps.append({"mixin": np.ascontiguousarray(mix),
                     "xs": np.ascontiguousarray(inp["x"][b, rows]),
                     "w_out": np.ascontiguousarray(inp["w_out"][0]),
                     "w_up": np.ascontiguousarray(inp["w_up"][0]),
                     "w_down": np.ascontiguousarray(inp["w_down"][0]),
                     "gpost": np.ascontiguousarray(inp["norm_post_mix"][0]),
                     "gpm2": np.ascontiguousarray(inp["norm_pre_mlp"][0].reshape(16, 128).T),
                     "gpost2": np.ascontiguousarray(inp["norm_post_mlp"][0]),
                     "ident": ident})
    return maps


def kernel_fused(**inp):
    inp = {k_: np.asarray(v) for k_, v in inp.items()}
    nc = _fused_program()
    res = run_bass_kernel_spmd(nc, _fused_inputs(inp), core_ids=list(range(8)))
    out = np.empty((2, SEQ, D), np.float32)
    for r in range(8):
        b, q = r // 4, r % 4
        out[b, q * 1024:(q + 1) * 1024] = res.results[r]["out"]
    return out


def _fused_program(nblk=NB):
    nc = bass.Bass("TRN2", target_bir_lowering=False)
    def din(name, shape, dt=F32):
        return nc.dram_tensor(name, shape, dt, kind="ExternalInput").ap()
    x = din("x", [SEQ, D])
    w_in = din("w_in", [4, D, WC])
    cvec = {"gpm": din("gpm", [128, 16]), "cw": din("cw", [4, 128, 6, 4]), "dtb8": din("dtb8", [4, 8]),
            "alog8": din("alog8", [4, 8]), "gnw": din("gnw", [128]), "sbw": din("sbw", [128]),
            "lq1": din("lq1", [64]), "lk1": din("lk1", [64]), "lq2": din("lq2", [64]), "lk2": din("lk2", [64])}
    ctab = {"cos": din("cos", [128, SEQ]), "sin": din("sin", [128, SEQ])}
    cmat = {"ident": din("ident", [128, 128], BF16), "U": din("U", [128, 128]), "SU": din("SU", [128, 128]),
            "Ub": din("Ub", [128, 128], BF16), "ONES": din("ONES", [128, 128]), "IDF": din("IDF", [128, 128])}
    xs = din("xs", [NT * 128, D])
    w_out = din("w_out", [D, D])
    w_up = din("w_up", [D, 4 * D])
    w_down = din("w_down", [4 * D, D])
    gpost = din("gpost", [D])
    gpm2 = din("gpm2", [128, 16])
    gpost2 = din("gpost2", [D])
    out = nc.dram_tensor("out", [NT * 128, D], F32, kind="ExternalOutput").ap()
    vmask = din("vmask", [128, 32])
    ho4 = nc.dram_tensor("ho4", [4, NT * 128, 512], BF16).ap()
    with contextlib.ExitStack() as top:
        k = K(nc, top)
        with contextlib.ExitStack() as esA:
            k.es = esA
            build_phase_a(nc, esA, k, x, w_in, cvec, ctab, cmat, ho4, nblk=nblk, npass=4, own_from=nblk - 2,
                          vmask=vmask)
        k.S.barrier()
        k.outs = []
        with contextlib.ExitStack() as esB:
            k.es = esB
            build_phase_b(nc, esB, k, None, xs, w_out, w_up, w_down, gpost, gpm2, gpost2, cmat["ident"], out,
                          ho4=ho4)
        k.S.wait_all("sp", k.outs)
        k.S.emit()
    return nc


def _fused_inputs(inp):
    a_maps = _phase_a_inputs(inp)
    maps = []
    perm = np.asarray([half * 1024 + g * 256 + j for g in range(4) for half in range(2) for j in range(256)])
    w_out_p = np.ascontiguousarray(inp["w_out"][0][perm])
    for r in range(8):
        b, q = r // 4, r % 4
        m = dict(a_maps[4 * b])
        for nm in ("w_in", "cw", "dtb8", "alog8"):
            m[nm] = np.ascontiguousarray(np.concatenate([a_maps[4 * b + g][nm] for g in range(4)], axis=0))
        rows = slice(q * 1024, (q + 1) * 1024)
        pad = (3 - q) * 1024
        xsh = np.zeros((SEQ, D), np.float32)
        xsh[pad:] = inp["x"][b, :(q + 1) * 1024]
        m["x"] = xsh
        for nm in ("cos", "sin"):
            tab = np.zeros((128, SEQ), np.float32)
            tab[:, pad:] = a_maps[0][nm][:, :SEQ - pad]
            m[nm] = tab
        vmk = np.zeros((128, 32), np.float32)
        vmk[:, pad // 128:] = 1.0
        m["vmask"] = vmk
        m.update({"xs": np.ascontiguousarray(inp["x"][b, rows]),
                  "w_out": w_out_p,
                  "w_up": np.ascontiguousarray(inp["w_up"][0]),
                  "w_down": np.ascontiguousarray(inp["w_down"][0]),
                  "gpost": np.ascontiguousarray(inp["norm_post_mix"][0]),
                  "gpm2": np.ascontiguousarray(inp["norm_pre_mlp"][0].reshape(16, 128).T),
                  "gpost2": np.ascontiguousarray(inp["norm_post_mlp"][0])})
        maps.append(m)
    return maps


def kernel_unfused(**inp):
    inp = {k_: np.asarray(v) for k_, v in inp.items()}
    hos = run_phase_a(inp)
    ncb = _phase_b_program()
    res = run_bass_kernel_spmd(ncb, _phase_b_inputs(inp, hos), core_ids=list(range(8)))
    out = np.empty((2, SEQ, D), np.float32)
    for r in range(8):
        b, q = r // 4, r % 4
        out[b, q * 1024:(q + 1) * 1024] = res.results[r]["out"]
    return out


def kernel(**inp):
    return kernel_unfused(**inp)
```
